# Optimizing a Trainium2 kernel written in Bass

```python
import jax, jax.numpy as jnp
from jax import lax
import numpy as np

D_MODEL = 1024
BATCH = 16
SEQ = 256
DEPTH = 2
DEC_BATCH = 2
DEC_SEQ = 2048
PAST_LEN = 256

GRID_W = 64
ATTN_BLOCK = 128
WINDOW = 128
SGU_CHUNK = 128
RET_CHUNK = 128
HEAD_DIM = 64
GROUP_WIDTH = D_MODEL // 4
A_HEADS = GROUP_WIDTH // HEAD_DIM
A_KV_HEADS = A_HEADS // 2
A_GROUP = A_HEADS // A_KV_HEADS
B_HEADS = 4
B_HEAD_DIM = GROUP_WIDTH // B_HEADS
C_HEADS = 4
C_HEAD_DIM = GROUP_WIDTH // C_HEADS
D_GROUPS = 4
D_GROUP_DIM = GROUP_WIDTH // D_GROUPS
POOL_WINDOWS = (2, 4, 8, 16)
D_FF = 2816
CONV_W = 3
ROPE_BASE = 10000.0
LN_EPS = 1e-5
NEG_INF = -1e30
SPLIT_SIZES = (GROUP_WIDTH, A_KV_HEADS * HEAD_DIM, A_KV_HEADS * HEAD_DIM,
               GROUP_WIDTH, GROUP_WIDTH,
               GROUP_WIDTH, GROUP_WIDTH, GROUP_WIDTH, GROUP_WIDTH, GROUP_WIDTH,
               GROUP_WIDTH)
IN_WIDTH = sum(SPLIT_SIZES)

kernel_name = "hybrid_prefix_diffusion_step"

F32 = jnp.float32


def layer_norm(x, w, b):
    xf = x.astype(F32)
    mu = xf.mean(-1, keepdims=True)
    var = jnp.square(xf - mu).mean(-1, keepdims=True)
    return ((xf - mu) * lax.rsqrt(var + LN_EPS) * w.astype(F32) + b.astype(F32)).astype(x.dtype)


def modulation(cond, w, b):
    m = jax.nn.silu(cond) @ w + b
    return jnp.split(m[..., None, :], 6, axis=-1)


def project(h, w_in):
    z = h @ w_in
    points, acc = [], 0
    for s in SPLIT_SIZES[:-1]:
        acc += s
        points.append(acc)
    return jnp.split(z, points, axis=-1)


def axial_rope(x):
    n = x.shape[1]
    rows = n // GRID_W
    r, col = jnp.meshgrid(jnp.arange(rows), jnp.arange(GRID_W), indexing="ij")
    half = HEAD_DIM // 2
    freqs = ROPE_BASE ** (-jnp.arange(0, half, 2, dtype=F32) / half)

    def rot(xa, pos):
        ang = pos.reshape(-1).astype(F32)[:, None] * freqs[None, :]
        cos = jnp.cos(ang)[None, :, None, :].astype(x.dtype)
        sin = jnp.sin(ang)[None, :, None, :].astype(x.dtype)
        x1, x2 = jnp.split(xa, 2, axis=-1)
        return jnp.concatenate([x1 * cos - x2 * sin, x1 * sin + x2 * cos], axis=-1)

    return jnp.concatenate([rot(x[..., :half], r), rot(x[..., half:], col)], axis=-1)


def softmax_with_sink(s, sink):
    col = jnp.broadcast_to(sink.astype(F32)[:, :, None, None], s.shape[:-1] + (1,))
    return jax.nn.softmax(jnp.concatenate([s, col], axis=-1), axis=-1)[..., :-1]


def attn_context(q, k, v, sink):
    b, l = q.shape[:2]
    nb = l // ATTN_BLOCK
    scale = HEAD_DIM ** -0.5
    qb = q.reshape(b, nb, ATTN_BLOCK, A_KV_HEADS, A_GROUP, HEAD_DIM).swapaxes(0, 1)

    def one_block(qi):
        s = jnp.einsum("bqkgd,bjkd->bkgqj", qi, k).astype(F32) * scale
        p = softmax_with_sink(s, sink)
        return jnp.einsum("bkgqj,bjkd->bqkgd", p.astype(v.dtype), v)

    o = lax.map(one_block, qb)
    return o.swapaxes(0, 1).reshape(b, l, A_HEADS * HEAD_DIM)


def attn_latent(q, k, v, ck, cv, sink):
    b, n = q.shape[:2]
    nb = n // ATTN_BLOCK
    scale = HEAD_DIM ** -0.5
    qb = q.reshape(b, nb, ATTN_BLOCK, A_KV_HEADS, A_GROUP, HEAD_DIM)

    def band(a):
        ap = jnp.pad(a, ((0, 0), (ATTN_BLOCK, ATTN_BLOCK), (0, 0), (0, 0)))
        ap = ap.reshape(b, nb + 2, ATTN_BLOCK, A_KV_HEADS, HEAD_DIM)
        return jnp.concatenate([ap[:, :-2], ap[:, 1:-1], ap[:, 2:]], axis=2)

    kw, vw = band(k), band(v)
    s_loc = jnp.einsum("bnqkgd,bnjkd->bnkgqj", qb, kw).astype(F32) * scale
    blk = jnp.arange(nb)[:, None, None]
    qpos = blk * ATTN_BLOCK + jnp.arange(ATTN_BLOCK)[None, :, None]
    kpos = (blk - 1) * ATTN_BLOCK + jnp.arange(3 * ATTN_BLOCK)[None, None, :]
    valid = (kpos >= 0) & (kpos < n) & (jnp.abs(kpos - qpos) <= WINDOW)
    s_loc = jnp.where(valid[None, :, None, None], s_loc, NEG_INF)
    s_ctx = jnp.einsum("bnqkgd,bjkd->bnkgqj", qb, ck).astype(F32) * scale
    p = softmax_with_sink(jnp.concatenate([s_loc, s_ctx], axis=-1), sink).astype(v.dtype)
    o = (jnp.einsum("bnkgqj,bnjkd->bnqkgd", p[..., :3 * ATTN_BLOCK], vw)
         + jnp.einsum("bnkgqj,bjkd->bnqkgd", p[..., 3 * ATTN_BLOCK:], cv))
    return o.reshape(b, n, A_HEADS * HEAD_DIM)


def spatial_gating(u, v, nw, nb_, ws, bs):
    b, n, _ = v.shape
    v = layer_norm(v, nw, nb_)
    vb = v.reshape(b, n // SGU_CHUNK, SGU_CHUNK, B_HEADS, B_HEAD_DIM)
    s = jnp.einsum("hij,bcjhd->bcihd", ws, vb) + bs.T[:, :, None]
    return u * s.reshape(b, n, GROUP_WIDTH)


def retention_scan(q, k, v, log_g, s0):
    b, n = q.shape[:2]
    nc = n // RET_CHUNK
    i = jnp.arange(RET_CHUNK, dtype=F32)
    rel = i[:, None] - i[None, :]
    dmat = jnp.where(rel >= 0, jnp.exp(jnp.maximum(rel, 0.0)[None] * log_g[:, None, None]), 0.0)
    q_dec = jnp.exp((i + 1.0)[:, None] * log_g[None, :])[:, :, None]
    k_dec = jnp.exp((RET_CHUNK - 1.0 - i)[:, None] * log_g[None, :])[:, :, None]
    c_dec = jnp.exp(RET_CHUNK * log_g)[:, None, None]
    chunks = tuple(a.reshape(b, nc, RET_CHUNK, C_HEADS, C_HEAD_DIM).swapaxes(0, 1) for a in (q, k, v))

    def step(state, inp):
        qc, kc, vc = inp
        sc = jnp.einsum("bihd,bjhd->bhij", qc, kc) * dmat
        o = (jnp.einsum("bhij,bjhe->bihe", sc, vc)
             + jnp.einsum("bihd,bhde->bihe", qc * q_dec, state))
        state = state * c_dec + jnp.einsum("bjhd,bjhe->bhde", kc * k_dec, vc)
        return state, o

    state, o = lax.scan(step, s0, chunks)
    return o.swapaxes(0, 1).reshape(b, n, C_HEADS, C_HEAD_DIM), state


def head_group_norm(o, w):
    mu = o.mean(-1, keepdims=True)
    var = jnp.square(o - mu).mean(-1, keepdims=True)
    y = (o - mu) * lax.rsqrt(var + LN_EPS) * w.astype(F32).reshape(C_HEADS, C_HEAD_DIM)
    return y.reshape(o.shape[0], o.shape[1], GROUP_WIDTH)


def retention(q, k, v, g_f, g_b, decay, gn_w, s0_f, s0_b):
    b, n = q.shape[:2]
    heads = lambda a: a.astype(F32).reshape(b, n, C_HEADS, C_HEAD_DIM)
    qf, kf, vf = heads(q), heads(k) * (C_HEAD_DIM ** -0.5), heads(v)
    log_g = jax.nn.log_sigmoid(decay.astype(F32))
    o_f, s_f = retention_scan(qf, kf, vf, log_g[0], s0_f.astype(F32))
    o_b, s_b = retention_scan(qf[:, ::-1], kf[:, ::-1], vf[:, ::-1], log_g[1], s0_b.astype(F32))
    o_b = o_b[:, ::-1]
    y = (jax.nn.silu(g_f.astype(F32)) * head_group_norm(o_f, gn_w[0])
         + jax.nn.silu(g_b.astype(F32)) * head_group_norm(o_b, gn_w[1]))
    return y.astype(q.dtype), s_f, s_b


def multiscale_pool(p, w, scale):
    b, n, _ = p.shape
    pf = p.astype(F32)
    cs = jnp.concatenate([jnp.zeros((b, 1, GROUP_WIDTH), F32), jnp.cumsum(pf, axis=1)], axis=1)
    t = jnp.arange(n)
    outs = []
    for g, win in enumerate(POOL_WINDOWS):
        lo = jnp.clip(t - win // 2, 0, n)
        hi = jnp.clip(t + win // 2, 0, n)
        sl = slice(g * D_GROUP_DIM, (g + 1) * D_GROUP_DIM)
        mean = (cs[:, hi, sl] - cs[:, lo, sl]) / (hi - lo).astype(F32)[None, :, None]
        outs.append(mean - pf[:, :, sl])
    y = jnp.stack(outs, axis=2)
    y = jnp.einsum("bngc,gcd->bngd", y, w.astype(F32)).reshape(b, n, GROUP_WIDTH)
    return (y * scale.astype(F32)).astype(p.dtype)


def conv_ffn(h, up, cw, cb, down):
    u = h @ up
    up_ = jnp.pad(u, ((0, 0), (1, 1), (0, 0)))
    u = up_[:, :-2] * cw[0] + up_[:, 1:-1] * cw[1] + up_[:, 2:] * cw[2] + cb
    a, g = jnp.split(u, 2, axis=-1)
    return (jax.nn.silu(a) * g) @ down


def trunk_layer(x, cond, ctx_k, ctx_v, s0_f, s0_b, lp):
    (w_ada, b_ada, w_in, w_out, attn_sink, sgu_nw, sgu_nb, sgu_ws, sgu_bs, ret_decay, ret_gn_w,
     pool_w, pool_scale, ffn_up, ffn_conv_w, ffn_conv_b, ffn_down, ln_w, ln_b) = lp
    alpha = (2.0 * DEPTH) ** 0.25
    sh_a, sc_a, g_a, sh_f, sc_f, g_f = modulation(cond, w_ada, b_ada)
    bsz, n = x.shape[:2]
    h = x * (1 + sc_a) + sh_a
    qa, ka, va, ub, vb, qc, kc, vc, gcf, gcb, pd = project(h, w_in)
    qa = qa.reshape(bsz, n, A_HEADS, HEAD_DIM)
    ka = ka.reshape(bsz, n, A_KV_HEADS, HEAD_DIM)
    va = va.reshape(bsz, n, A_KV_HEADS, HEAD_DIM)
    sink = attn_sink.reshape(A_KV_HEADS, A_GROUP)
    if ctx_k is None:
        y_a = attn_context(qa.reshape(bsz, n, A_KV_HEADS, A_GROUP, HEAD_DIM), ka, va, sink)
        k_out = ka
    else:
        qr = axial_rope(qa).reshape(bsz, n, A_KV_HEADS, A_GROUP, HEAD_DIM)
        k_out = axial_rope(ka)
        y_a = attn_latent(qr, k_out, va, ctx_k, ctx_v, sink)
    y_b = spatial_gating(ub, vb, sgu_nw, sgu_nb, sgu_ws, sgu_bs)
    y_c, s_f, s_b = retention(qc, kc, vc, gcf, gcb, ret_decay, ret_gn_w, s0_f, s0_b)
    y_d = multiscale_pool(pd, pool_w, pool_scale)
    mix = jnp.concatenate([y_a, y_b, y_c, y_d], axis=-1) @ w_out
    x = layer_norm(alpha * x + g_a * mix, ln_w[0], ln_b[0])
    h = x * (1 + sc_f) + sh_f
    x = layer_norm(alpha * x + g_f * conv_ffn(h, ffn_up, ffn_conv_w, ffn_conv_b, ffn_down), ln_w[1], ln_b[1])
    return x, k_out, va, s_f, s_b


def setup_inputs(seed: int = 0) -> dict:
    key = jax.random.key(seed)
    ks = jax.random.split(key, 26)
    nrm = lambda k, shape, s: jax.random.normal(k, shape, F32) * s
    beta = (8.0 * DEPTH) ** -0.25
    gam = np.stack([1.0 - 2.0 ** (-5.0 - np.arange(C_HEADS)),
                    1.0 - 2.0 ** (-5.5 - np.arange(C_HEADS))])
    decay0 = jnp.asarray(np.log(gam / (1.0 - gam)), F32)
    return {
        "x_prompt": nrm(ks[0], (BATCH, SEQ, D_MODEL), 1.0),
        "x_sample": nrm(ks[1], (DEC_BATCH, DEC_SEQ, D_MODEL), 1.0),
        "cache_attn_k": nrm(ks[2], (DEC_BATCH, DEPTH, PAST_LEN, A_KV_HEADS, HEAD_DIM), 1.0),
        "cache_attn_v": nrm(ks[3], (DEC_BATCH, DEPTH, PAST_LEN, A_KV_HEADS, HEAD_DIM), 1.0),
        "state_ret": nrm(ks[4], (DEC_BATCH, DEPTH, 2, C_HEADS, C_HEAD_DIM, C_HEAD_DIM), 0.5),
        "c": nrm(ks[5], (DEC_BATCH, D_MODEL), 1.0),
        "c_ctx": nrm(ks[6], (D_MODEL,), 1.0),
        "w_ada": nrm(ks[7], (DEPTH, D_MODEL, 6 * D_MODEL), D_MODEL ** -0.5),
        "b_ada": nrm(ks[8], (DEPTH, 6 * D_MODEL), 0.02),
        "w_in": nrm(ks[9], (DEPTH, D_MODEL, IN_WIDTH), D_MODEL ** -0.5),
        "w_out": nrm(ks[10], (DEPTH, D_MODEL, D_MODEL), beta * D_MODEL ** -0.5),
        "attn_sink": nrm(ks[11], (DEPTH, A_HEADS), 0.5),
        "sgu_norm_w": 1.0 + nrm(ks[12], (DEPTH, GROUP_WIDTH), 0.02),
        "sgu_norm_b": nrm(ks[13], (DEPTH, GROUP_WIDTH), 0.02),
        "sgu_ws": nrm(ks[14], (DEPTH, B_HEADS, SGU_CHUNK, SGU_CHUNK), SGU_CHUNK ** -0.5),
        "sgu_bs": 1.0 + nrm(ks[15], (DEPTH, B_HEADS, SGU_CHUNK), 0.02),
        "ret_decay": decay0[None] + nrm(ks[16], (DEPTH, 2, C_HEADS), 0.01),
        "ret_gn_w": 1.0 + nrm(ks[17], (DEPTH, 2, GROUP_WIDTH), 0.02),
        "pool_w": nrm(ks[18], (DEPTH, D_GROUPS, D_GROUP_DIM, D_GROUP_DIM), D_GROUP_DIM ** -0.5),
        "pool_scale": 1.0 + nrm(ks[19], (DEPTH, GROUP_WIDTH), 0.02),
        "ffn_up": nrm(ks[20], (DEPTH, D_MODEL, 2 * D_FF), D_MODEL ** -0.5),
        "ffn_conv_w": nrm(ks[21], (DEPTH, CONV_W, 2 * D_FF), CONV_W ** -0.5),
        "ffn_conv_b": nrm(ks[22], (DEPTH, 2 * D_FF), 0.02),
        "ffn_down": nrm(ks[23], (DEPTH, D_FF, D_MODEL), beta * D_FF ** -0.5),
        "ln_w": 1.0 + nrm(ks[24], (DEPTH, 2, D_MODEL), 0.02),
        "ln_b": nrm(ks[25], (DEPTH, 2, D_MODEL), 0.02),
    }


def reference(x_prompt, x_sample, cache_attn_k, cache_attn_v, state_ret, c, c_ctx,
              w_ada, b_ada, w_in, w_out, attn_sink, sgu_norm_w, sgu_norm_b, sgu_ws, sgu_bs,
              ret_decay, ret_gn_w, pool_w, pool_scale, ffn_up, ffn_conv_w, ffn_conv_b, ffn_down,
              ln_w, ln_b):
    def layer_params(l):
        return (w_ada[l], b_ada[l], w_in[l], w_out[l], attn_sink[l], sgu_norm_w[l], sgu_norm_b[l],
                sgu_ws[l], sgu_bs[l], ret_decay[l], ret_gn_w[l], pool_w[l], pool_scale[l],
                ffn_up[l], ffn_conv_w[l], ffn_conv_b[l], ffn_down[l], ln_w[l], ln_b[l])

    xp, xs = x_prompt, x_sample
    zero_state = jnp.zeros((x_prompt.shape[0], C_HEADS, C_HEAD_DIM, C_HEAD_DIM), F32)
    new_k, new_v, new_s = [], [], []
    for l in range(DEPTH):
        lp = layer_params(l)
        xp, ka, va, s_f, s_b = trunk_layer(xp, c_ctx[None], None, None, zero_state, zero_state, lp)
        new_k.append(ka)
        new_v.append(va)
        new_s.append(jnp.stack([s_f, s_b], axis=1).astype(x_prompt.dtype))
        xs, _, _, _, _ = trunk_layer(xs, c, cache_attn_k[:, l], cache_attn_v[:, l],
                                     state_ret[:, l, 0], state_ret[:, l, 1], lp)
    new_cache_attn_k = jnp.stack(new_k, axis=1)
    new_cache_attn_v = jnp.stack(new_v, axis=1)
    new_state_ret = jnp.stack(new_s, axis=1)
    return (xp, xs, new_cache_attn_k, new_cache_attn_v, new_state_ret)
```

```python
import os
import numpy as np
import concourse.bass as bass
import concourse.mybir as mybir
from concourse.bass_utils import run_bass_kernel_spmd

F32 = mybir.dt.float32
BF16 = mybir.dt.bfloat16
AF = mybir.ActivationFunctionType
ALU = mybir.AluOpType

D = 1024
KC = 8
DEPTH = 2
NCORE = 8
DFF = 2816
NFC = 22
ALPHA = float((2.0 * DEPTH) ** 0.25)
EPS = 1e-5
WIN = 15 * 256
NSLOT = 3
SLOTW = 2816
NDSEM = {"sp": 10, "pool": 4}


class Sched:
    ENG = ("pe", "act", "dve", "pool", "sp")

    def __init__(self):
        self.q = {e: [] for e in self.ENG}
        self.res = {}
        self.dma_rr = {e: 0 for e in self.ENG}
        self.dma_cnt = {}
        self.last_ev = {}
        self.pending = {}

    def add(self, eng, fn, r=(), w=(), dma=False):
        idx = len(self.q[eng])
        deps = set()
        for k in r:
            e = self.res.get(k)
            if e is not None and e[0] is not None:
                deps.add(e[0])
            if e is not None and k.startswith("ps"):
                for rd in e[1]:
                    if not (rd[0] == "E" and rd[1] == eng):
                        deps.add(rd)
        raw = set(deps)
        for k in w:
            e = self.res.get(k)
            if e is not None:
                if e[0] is not None:
                    deps.add(e[0])
                deps.update(e[1])
        if eng in self.pending:
            deps.update(self.pending.pop(eng))
        if dma:
            sk = (eng, self.dma_rr[eng] % NDSEM[eng])
            self.dma_rr[eng] += 1
            prev = self.dma_cnt.get(sk, 0)
            if prev > 0:
                deps.add(("D", sk, prev))
            cnt = prev + 16
            self.dma_cnt[sk] = cnt
            ev = ("D", sk, cnt)
            self.last_ev[sk] = ev
        else:
            ev = ("E", eng, idx)
            self.last_ev[eng] = ev
        d2 = []
        for d in deps:
            if d[0] == "E" and d[1] == eng and eng == "pe":
                continue
            d2.append(d)
        self.q[eng].append((fn, d2, ev, dma))
        for k in r:
            self.res.setdefault(k, [None, []])[1].append(ev)
        for k in w:
            self.res[k] = [ev, []]
        return ev

    def barrier(self):
        evs = set(self.last_ev.values())
        for e in self.ENG:
            if e == "pool":
                continue
            self.pending.setdefault(e, set()).update(evs)

    def emit(self, nc, block, semobj):
        needed = set()
        for e in self.ENG:
            for (_, deps, _, _) in self.q[e]:
                for d in deps:
                    if d[0] == "E":
                        needed.add(d)
        lastc = {}
        for e in self.ENG:
            for (_, _, ev, dma) in self.q[e]:
                if not dma:
                    lastc[e] = ev
        needed.update(lastc.values())
        val = {}
        for e in self.ENG:
            c = 0
            for i, (_, _, ev, dma) in enumerate(self.q[e]):
                if (not dma) and ev in needed:
                    c += 1
                    val[ev] = c
        final = dict(self.dma_cnt)

        def run(eng, h):
            waited = {}
            for (fn, deps, ev, dma) in self.q[eng]:
                ws = {}
                for d in deps:
                    if d[0] == "E":
                        sk, v = d[1], val[d]
                    else:
                        sk, v = d[1], d[2]
                    if v > ws.get(sk, 0):
                        ws[sk] = v
                for sk, v in ws.items():
                    if waited.get(sk, 0) >= v:
                        continue
                    h.wait_ge(semobj[sk], v)
                    waited[sk] = v
                ins = fn(h)
                if dma:
                    ins.then_inc(semobj[ev[1]], 16)
                elif ev in needed:
                    ins.then_inc(semobj[eng], 1)
            if eng == "sp":
                for sk, v in final.items():
                    if waited.get(sk, 0) < v:
                        h.wait_ge(semobj[sk], v)
                for e2, ev2 in lastc.items():
                    if e2 != "sp":
                        h.wait_ge(semobj[e2], val[ev2])

        @block.tensor
        def _(h):
            run("pe", h)

        @block.scalar
        def _(h):
            run("act", h)

        @block.vector
        def _(h):
            run("dve", h)

        @block.gpsimd
        def _(h):
            run("pool", h)

        @block.sync
        def _(h):
            run("sp", h)


def _perm64():
    d = np.arange(64)
    return np.where((d % 32) < 16, d + 16, d - 16)


def _win_ext_cols():
    qa, ka, va, ub, vb, qc, kc, vc, gcf, gcb, pd = (0, 256, 384, 512, 768, 1024, 1280, 1536, 1792, 2048, 2304)
    p = _perm64()
    r = np.arange
    cols = []
    cols += list(ka + r(128))
    cols += list(ka + np.concatenate([p, 64 + p]))
    cols += list(pd + r(256))
    cols += list(va + r(128)) + list(ka + r(128))
    cols += list(kc + r(256))
    cols += list(vc + r(256))
    qg = [np.concatenate([qa + (0 * 2 + g) * 64 + r(64), qa + (1 * 2 + g) * 64 + r(64)]) for g in (0, 1)]
    qgp = [np.concatenate([qa + (0 * 2 + g) * 64 + p, qa + (1 * 2 + g) * 64 + p]) for g in (0, 1)]
    cols += list(qg[0]) + list(qg[1])
    cols += list(qgp[0]) + list(qgp[1])
    cols += list(ub + r(256))
    cols += list(qc + r(256))
    cols += list(kc + r(256))
    cols += list(gcf + r(256))
    cols += list(gcb + r(256))
    cols += list(vb + r(256))
    cols += list(kc + r(256))
    cols += list(vc + r(256))
    cols = np.asarray(cols)
    assert cols.shape[0] == WIN
    return cols


def _up_ext_cols():
    cols = []
    for i in range(NFC):
        cols += list(i * 128 + np.arange(128)) + list(DFF + i * 128 + np.arange(128))
    return np.asarray(cols)


def _fm(v, nch):
    return np.ascontiguousarray(v.reshape(nch, 128).T)


def _const_tables():
    j = np.arange(128, dtype=np.float32)[:, None]
    i = np.arange(128, dtype=np.float32)[None, :]
    c = {}
    c["relF"] = np.maximum(i - j, 0.0)
    c["relB"] = np.maximum(j - i, 0.0)
    c["maskF8"] = (i >= j).astype(np.float32) * 0.125
    c["maskB8"] = (j >= i).astype(np.float32) * 0.125
    c["iota1"] = np.broadcast_to(i + 1.0, (128, 128)).copy()
    c["iotaB"] = np.broadcast_to(128.0 - i, (128, 128)).copy()
    big = np.concatenate([c["relF"], c["relB"], c["maskF8"], c["maskB8"], c["iota1"], c["iotaB"]], axis=1)
    small = np.zeros((128, 64), np.float32)
    small[:, 0] = 127.0 - np.arange(128)
    small[:, 1] = np.arange(128)
    wins = np.array([[2, 4], [8, 16]], np.float32)
    for c2 in range(2):
        for half in range(2):
            w = wins[c2, half]
            rows = slice(half * 64, (half + 1) * 64)
            small[rows, 2 + c2] = 1.0 / w
            for e in range(8):
                cnt_s = min(w, e + w / 2)
                small[rows, 4 + c2 * 8 + e] = w / cnt_s
                dist = 8 - e
                cnt_e = (dist + w / 2) if dist < w / 2 else w
                small[rows, 20 + c2 * 8 + e] = w / cnt_e
    bd = np.zeros((128, 128), np.float32)
    bd[:64, :64] = 1.0
    bd[64:, 64:] = 1.0
    jj = np.arange(128)[:, None]
    ii = np.arange(128)[None, :]
    mprev = (jj >= ii).astype(np.float32)
    mnext = (jj <= ii).astype(np.float32)
    cb = np.concatenate([np.full((128, 128), 1.0 / 1024, np.float32), bd / 64.0, np.ones((128, 64), np.float32),
                         mprev, mprev, mnext, mnext], axis=1)
    bd2 = np.concatenate([bd, bd], axis=1)
    return big, small, bd2, cb


def _rope_tables():
    n = 2048
    t = np.arange(n)
    row = (t // 64).astype(np.float32)
    col = (t % 64).astype(np.float32)
    half = 32
    freqs = (10000.0 ** (-np.arange(0, half, 2, dtype=np.float32) / half)).astype(np.float32)
    C = np.zeros((64, n), np.float32)
    S = np.zeros((64, n), np.float32)
    for d in range(64):
        pos = row if d < 32 else col
        f = freqs[d % 16]
        ang = (pos * f).astype(np.float32)
        C[d] = np.cos(ang)
        S[d] = np.sin(ang) * (-1.0 if (d % 32) < 16 else 1.0)
    return np.concatenate([C, C], 0), np.concatenate([S, S], 0)


class Group:
    def __init__(self, name, ntile, seqlen, ci, sample):
        self.name = name
        self.ntile = ntile
        self.nt = ntile * 512
        self.seqlen = seqlen
        self.nseq = self.nt // seqlen
        self.ci = ci
        self.sample = sample
        self.nchunk = ntile * 4
        self.cps = seqlen // 128

    def segs(self, t):
        out = []
        a = t * 512
        while a < (t + 1) * 512:
            s = a // self.seqlen
            e = min((s + 1) * self.seqlen, (t + 1) * 512)
            out.append((a - t * 512, e - a, s, a - s * self.seqlen))
            a = e
        return out


class _Stop(Exception):
    pass


def build_program(debug=None, stop=None):
    nc = bass.Bass("TRN2", target_bir_lowering=False)
    S = Sched()

    def din(name, shape):
        return nc.dram_tensor(name, list(shape), F32, kind="ExternalInput").ap()

    def dout(name, shape):
        return nc.dram_tensor(name, list(shape), F32, kind="ExternalOutput").ap()

    d_xp = din("xpT", [D, 512])
    d_xs = din("xsT", [D, 2048])
    d_cond = din("cond", [128, KC * 2])
    d_wada = din("w_ada", [DEPTH, D, 6 * D])
    d_bada = din("b_adaT", [128, DEPTH * 48])
    d_win = din("w_in_ext", [DEPTH, D, WIN])
    d_wout = din("w_out", [DEPTH, D, D])
    d_up = din("ffn_up_ext", [DEPTH, D, 2 * DFF])
    d_down = din("ffn_down", [DEPTH, DFF, D])
    d_conv = din("convT", [128, DEPTH * 4 * 44])
    d_ln = din("lnT", [128, DEPTH * 2 * 2 * KC])
    d_sink = din("sink", [1, DEPTH * 4])
    d_nwb = din("sgu_nwb", [DEPTH, 512])
    d_wsT = din("wsT", [128, DEPTH * 4 * 128])
    d_bsT = din("bsT", [64, DEPTH * 4 * 128])
    d_dec_row = din("dec_row", [1, DEPTH * 8])
    d_dec_col = din("dec_col", [128, DEPTH * 4])
    d_gnw = din("gnw", [128, DEPTH * 4])
    d_poolw = din("poolw", [DEPTH, 4, 64, 64])
    d_pscale = din("pool_scaleT", [128, DEPTH * 2])
    d_kctx = din("kctxT", [128, DEPTH * 256])
    d_vctx = din("vctx", [128, DEPTH * 2 * 128])
    d_st0 = din("state0", [DEPTH, 2, 4, 64, 64])
    d_cbig = din("c_big", [128, 768])
    d_csmall = din("c_small", [128, 64])
    d_bd2 = din("c_bd2", [128, 256])
    d_cb = din("c_bf", [128, 832])
    d_ropeC = din("ropeC", [128, 2048])
    d_ropeS = din("ropeS", [128, 2048])

    o_yp = dout("ypT", [D, 512])
    o_ys = dout("ysT", [D, 2048])
    o_ck = dout("ck_out", [DEPTH, 512, 128])
    o_cv = dout("cv_out", [DEPTH, 512, 128])
    o_st = dout("st_out", [DEPTH, 2, 2, 4, 64, 64])

    dbg = {}
    if debug:
        for name, shape in debug.items():
            dbg[name] = dout("dbg_" + name, shape)

    NF = 26410
    NB = 53568
    from contextlib import ExitStack
    es = ExitStack()
    fpool = es.enter_context(nc.sbuf_tensor("fpool", [128, NF], F32))
    bpool = es.enter_context(nc.sbuf_tensor("bpool", [128, NB], BF16))
    psum = es.enter_context(nc.psum_tensor("psum", [128, 4096], F32))
    semobj = {}
    for e in Sched.ENG:
        semobj[e] = es.enter_context(nc.semaphore("s_" + e))
        for i in range(NDSEM.get(e, 0)):
            semobj[(e, i)] = es.enter_context(nc.semaphore("d_%s%d" % (e, i)))
    block = es.enter_context(nc.Block())

    class Alloc:
        def __init__(self, t, size):
            self.t, self.size, self.off = t, size, 0

        def __call__(self, n, **dims):
            a = self.t[:, self.off:self.off + n]
            self.off += n
            assert self.off <= self.size, (self.off, self.size)
            return a

    fa = Alloc(fpool, NF)
    ba = Alloc(bpool, NB)

    def v3(ap, a):
        return ap.rearrange("p (a b) -> p a b", a=a)

    def v4(ap, a, b):
        return ap.rearrange("p (a b c) -> p a b c", a=a, b=b)

    X = v3(fa(KC * 2048), KC)
    csm = fa(64)
    bd2 = fa(256)
    modv = fa(DEPTH * 2 * 48)
    dvec = fa(DEPTH * 2 * 2 * KC)
    lvec = fa(DEPTH * 4 * KC)
    badaT = fa(DEPTH * 48)
    convT = fa(DEPTH * 4 * 44)
    lnT = fa(DEPTH * 2 * 2 * KC)
    sinkb = fa(DEPTH * 4)
    esink = fa(DEPTH * 4)
    decrow = fa(DEPTH * 8)
    lgrow = fa(DEPTH * 8)
    deccol = fa(DEPTH * 4)
    lgcol = fa(DEPTH * 4)
    gnw = fa(DEPTH * 4)
    pscale = fa(DEPTH * 2)
    condf = fa(KC * 2)
    nwb = fa(512)
    bsT = fa(512)
    DfT = fa(512)
    DbT = fa(512)
    qdecF = fa(256)
    qdecB = fa(256)
    kdec = fa(8)
    cdec = fa(4)
    SrunF = fa(256)
    SrunB = fa(256)
    S0 = fa(512)
    ropeC = fa(512)
    ropeS = fa(512)
    stg = fa(256)
    stgS = fa(128)
    hsave = fa(2 * NFC)
    lnst = fa(16)
    lnm = fa(512)
    lnv = fa(512)
    NFS = 5
    fscr = [fa(512) for _ in range(NFS)]
    fpers = fa.off

    wbuf = [ba(SLOTW) for _ in range(NSLOT)]
    cbf = ba(832)
    onesmean = cbf[:, 0:128]
    bd64 = cbf[:, 128:256]
    ones64 = cbf[:, 256:320]
    mprev = cbf[:, 320:576]
    mnext = cbf[:, 576:832]
    siluc = ba(KC * 2)
    wsT = ba(DEPTH * 4 * 128)
    poolW = ba(DEPTH * 2 * 128)
    kctx = ba(DEPTH * 256)
    vctx = ba(DEPTH * 2 * 128)
    NBS = 5
    bscr = [ba(512) for _ in range(NBS)]
    NER = 10
    ering = [ba(256) for _ in range(NER)]
    er_i = [0]
    sfr = [ba(256) for _ in range(4)]
    bpers = ba.off

    hA = v3(ba(KC * 512), KC)
    yT = hA
    kaT_g = ba(2048)
    va_g = v3(ba(16 * 128), 16)
    pd_g = v3(ba(2 * 2080), 2)
    Sb_g = v3(ba(16 * 256), 16)
    t2_off = ba.off
    qaT = v3(ba(2 * 512), 2)
    ubT = v3(ba(4 * 512), 4)
    qcT = v3(ba(2 * 512), 2)
    kcT = v3(ba(2 * 512), 2)
    qdf = v3(ba(2 * 512), 2)
    qdb = v3(ba(2 * 512), 2)
    sgf = v3(ba(2 * 512), 2)
    sgb = v3(ba(2 * 512), 2)
    vbn = v3(ba(4 * 256), 4)
    vz = ba(4 * 512)
    vcb = v3(ba(4 * 256), 4)
    kdf = v3(ba(4 * 256), 4)
    scf = [ba(512) for _ in range(2)]
    scb = [ba(512) for _ in range(2)]
    kdb1_flat = ba(4 * 256)
    vcb1_flat = ba(4 * 256)
    kdb1 = v3(kdb1_flat, 4)
    vcb1 = v3(vcb1_flat, 4)
    yA = v3(kdb1_flat, 2)
    yD = v3(vcb1_flat, 2)
    b1_end = ba.off
    ba.off = bpers
    hF = v3(ba(KC * 1024), KC)
    hFh = v3(ba(KC * 2), KC)
    hid = v3(ba(NFC * 1024), NFC)
    b2_end = ba.off
    assert max(b1_end, b2_end) <= NB, (b1_end, b2_end, NB)
    assert fpers <= NF, (fpers, NF)

    def PS(b, n0=0, n1=512):
        return psum[:, b * 512 + n0:b * 512 + n1]

    def pk(b):
        return "ps%d" % b

    class RR:
        def __init__(self, banks):
            self.banks, self.i = list(banks), 0

        def __call__(self):
            b = self.banks[self.i % len(self.banks)]
            self.i += 1
            return b

    rr_all = RR(range(8))
    rr_lo = RR(range(4))

    fs_i = [0]
    bs_i = [0]

    def fsc():
        i = fs_i[0] % NFS
        fs_i[0] += 1
        return fscr[i], "fs%d" % i

    def bsc():
        i = bs_i[0] % NBS
        bs_i[0] += 1
        return bscr[i], "bs%d" % i

    def MM(out, lhsT, rhs, start, stop, r, w):
        S.add("pe", lambda h: h.matmul(out, lhsT=lhsT, rhs=rhs, start=start, stop=stop), r, w)

    def ACT(out, in_, func, r, w, scale=None, bias=None):
        kw = {}
        if scale is not None:
            kw["scale"] = scale
        if bias is not None:
            kw["bias"] = bias
        S.add("act", lambda h: h.activation(out=out, in_=in_, func=func, **kw), r, w)

    def TT(out, a, b, op, r, w):
        S.add("dve", lambda h: h.tensor_tensor(out=out, in0=a, in1=b, op=op), r, w)

    def TS(out, a, s1, s2, op0, op1, r, w):
        if s2 is None:
            S.add("dve", lambda h: h.tensor_scalar(out=out, in0=a, scalar1=s1, scalar2=None, op0=op0), r, w)
        else:
            S.add("dve", lambda h: h.tensor_scalar(out=out, in0=a, scalar1=s1, scalar2=s2, op0=op0, op1=op1), r, w)

    def STT(out, a, s, b, op0, op1, r, w):
        S.add("dve", lambda h: h.scalar_tensor_tensor(out=out, in0=a, scalar=s, in1=b, op0=op0, op1=op1), r, w)

    def VCP(out, in_, r, w):
        S.add("dve", lambda h: h.tensor_scalar(out=out, in0=in_, scalar1=1.0, scalar2=None, op0=ALU.mult), r, w)

    def VMEMSET(ap, val, w):
        S.add("dve", lambda h: h.memset(ap, val), (), w)

    def DMA(out, in_, r, w, eng="sp"):
        S.add(eng, lambda h: h.dma_start(out=out, in_=in_), r, w, dma=True)

    wl_i = [0]

    def wload(src, kch, cols):
        i = wl_i[0] % NSLOT
        wl_i[0] += 1
        dst = wbuf[i][:, 0:kch * cols].rearrange("p (k c) -> p k c", k=kch)
        s = src.rearrange("(k p) c -> p k c", p=128)
        DMA(dst, s, (), ["w%d" % i], eng="pool")
        return dst, "w%d" % i

    def xk(kc, t):
        return "x%d_%d" % (kc, t)

    def hk(kc):
        return "h%d" % kc

    def ckpt(name):
        if stop == name:
            raise _Stop()

    try:
        DMA(csm, d_csmall, (), ["csm"])
        DMA(bd2, d_bd2, (), ["bd2"])
        DMA(cbf, d_cb, (), ["cbf"], eng="pool")
        DMA(badaT, d_bada, (), ["badaT"])
        DMA(convT, d_conv, (), ["convT"])
        DMA(lnT, d_ln, (), ["lnT"])
        DMA(sinkb, d_sink.partition_broadcast(128), (), ["sinkb"])
        DMA(decrow, d_dec_row.partition_broadcast(128), (), ["decrow"])
        DMA(deccol, d_dec_col, (), ["deccol"])
        DMA(gnw, d_gnw, (), ["gnw"])
        DMA(pscale, d_pscale, (), ["pscale"])
        DMA(condf, d_cond, (), ["condf"])
        DMA(wsT, d_wsT, (), ["wsT"], eng="pool")
        DMA(kctx, d_kctx, (), ["kctx"], eng="pool")
        DMA(vctx, d_vctx, (), ["vctx"], eng="pool")
        VMEMSET(poolW, 0.0, ["poolW"])
        pw4 = v4(poolW, DEPTH, 2)
        for l in range(DEPTH):
            for g4 in range(4):
                c2, gg = g4 // 2, g4 % 2
                DMA(pw4[gg * 64:(gg + 1) * 64, l, c2, gg * 64:(gg + 1) * 64], d_poolw[l, g4], (), ["poolW"], eng="pool")

        col127 = csm[:, 0:1]
        colj = csm[:, 1:2]
        invw = csm[:, 2:4]
        corrS = v3(csm[:, 4:20], 2)
        corrE = v3(csm[:, 20:36], 2)

        f, fk = fsc()
        ACT(f[:, 0:16], decrow, AF.Exp, ["decrow"], [fk], scale=-1.0)
        ACT(f[:, 16:32], f[:, 0:16], AF.Ln, [fk], [fk], bias=1.0)
        TS(lgrow, f[:, 16:32], -1.0, None, ALU.mult, ALU.bypass, [fk], ["lgrow"])
        ACT(f[:, 32:40], deccol, AF.Exp, ["deccol"], [fk], scale=-1.0)
        ACT(f[:, 40:48], f[:, 32:40], AF.Ln, [fk], [fk], bias=1.0)
        TS(lgcol, f[:, 40:48], -1.0, None, ALU.mult, ALU.bypass, [fk], ["lgcol"])
        ACT(esink, sinkb, AF.Exp, ["sinkb"], ["esink"])
        ACT(siluc, condf, AF.Silu, ["condf"], ["siluc"])
        siluc3 = v3(siluc, KC)
        ckpt("consts")

        modv4 = v4(modv, DEPTH, 2)
        dvec4 = v4(dvec, DEPTH, 2)
        lvec3 = v3(lvec, DEPTH)
        lnT5 = lnT.rearrange("p (l i j k) -> p l i j k", l=DEPTH, i=2, j=2)

        def do_mod(l):
            b = rr_all()
            for blk in range(24):
                w, wk = wload(d_wada[l][:, blk * 256:(blk + 1) * 256], KC, 256)
                for j in range(2):
                    n = blk * 2 + j
                    for kc in range(KC):
                        MM(PS(b, n * 2, n * 2 + 2), w[:, kc, j * 128:(j + 1) * 128], siluc3[:, kc, :], kc == 0, kc == KC - 1,
                           [wk, "siluc"], [pk(b)])
            pv = PS(b, 0, 96).rearrange("p (n c) -> p n c", c=2)
            for ci in range(2):
                TT(modv4[:, l, ci, :], pv[:, :, ci], badaT[:, l * 48:(l + 1) * 48], ALU.add, [pk(b), "badaT"], ["modv%d" % l])
            for ci in range(2):
                TS(dvec4[:, l, ci, 0:KC], modv4[:, l, ci, 8:16], 1.0, 1.0 / ALPHA, ALU.add, ALU.mult, ["modv%d" % l], ["dvec%d" % l])
                TS(dvec4[:, l, ci, KC:2 * KC], modv4[:, l, ci, 32:40], 1.0, 1.0 / ALPHA, ALU.add, ALU.mult, ["modv%d" % l], ["dvec%d" % l])
            TS(lvec3[:, l, 0:16], lnT5[:, l, 0].rearrange("p j k -> p (j k)"), ALPHA, None, ALU.mult, ALU.bypass, ["lnT"], ["lvec%d" % l])
            a2 = ALPHA if l < DEPTH - 1 else 1.0
            TS(lvec3[:, l, 16:32], lnT5[:, l, 1].rearrange("p j k -> p (j k)"), a2, None, ALU.mult, ALU.bypass, ["lnT"], ["lvec%d" % l])

        do_mod(0)
        ckpt("mod")

        def mvec(l, ci, q):
            return modv4[:, l, ci, q * 8:(q + 1) * 8]

        def make_h(dst, l, g, t, which, ncols=512, col0=None, dkeys=None):
            c0 = t * 512 if col0 is None else col0
            sv = dvec4[:, l, g.ci, 0:KC] if which == "a" else dvec4[:, l, g.ci, KC:2 * KC]
            bv = mvec(l, g.ci, 0 if which == "a" else 3)
            for kc in range(KC):
                tt = (c0 // 512)
                ACT(dst[:, kc, :], X[:, kc, c0:c0 + ncols], AF.Identity, [xk(kc, tt), "dvec%d" % l, "modv%d" % l],
                    [dkeys[kc] if dkeys else hk(kc)], scale=sv[:, kc:kc + 1], bias=bv[:, kc:kc + 1])

        def ln_stats(t, kc, bM, bQ):
            tc = slice(t * 512, (t + 1) * 512)
            tb, tbk = bsc()
            tq, tqk = bsc()
            ACT(tb, X[:, kc, tc], AF.Copy, [xk(kc, t)], [tbk])
            ACT(tq, X[:, kc, tc], AF.Square, [xk(kc, t)], [tqk])
            MM(PS(bM), onesmean, tb, kc == 0, kc == KC - 1, ["cbf", tbk], [pk(bM)])
            MM(PS(bQ), onesmean, tq, kc == 0, kc == KC - 1, ["cbf", tqk], [pk(bQ)])

        def ln_finish(t, wv, bv, bM, bQ):
            tc = slice(t * 512, (t + 1) * 512)
            fm, fmk = lnm, "lnm"
            fv, fvk = lnv, "lnv"
            ACT(fm, PS(bM), AF.Copy, [pk(bM)], [fmk])
            ACT(fv, PS(bM), AF.Square, [pk(bM)], [fvk])
            TT(fv, PS(bQ), fv, ALU.subtract, [pk(bQ), fvk], [fvk])
            ACT(fv, fv, AF.Ln, [fvk], [fvk], bias=EPS)
            ACT(fv, fv, AF.Exp, [fvk], [fvk], scale=-0.5)
            for kc in range(KC):
                u, uk = fsc()
                TT(u, X[:, kc, tc], fm, ALU.subtract, [xk(kc, t), fmk], [uk])
                TT(u, u, fv, ALU.mult, [uk, fvk], [uk])
                ACT(X[:, kc, tc], u, AF.Identity, [uk, "lvec0", "lvec1"], [xk(kc, t)], scale=wv[:, kc:kc + 1], bias=bv[:, kc:kc + 1])

        def dbg_out(name, ap, keys):
            if name in dbg:
                DMA(dbg[name], ap, keys, [])

        stgS3 = v3(stgS, 2)

        def state_out(S3, key, l, sq, dr):
            for pr in range(2):
                for hh in range(2):
                    TS(stgS3[hh * 64:(hh + 1) * 64, pr, :], S3[hh * 64:(hh + 1) * 64, pr, hh * 64:(hh + 1) * 64], 1.0, None,
                       ALU.mult, ALU.bypass, [key], ["stgS"])
            DMA(o_st[l, sq, dr].rearrange("(pr hh) d e -> (hh d) pr e", pr=2), stgS3, ["stgS"], [])

        groups = [Group("p", 1, 256, 0, False), Group("s", 4, 2048, 1, True)]
        vz5 = vz.rearrange("p (c a h e) -> p c a h e", c=4, a=2, h=2)
        wsT4 = v4(wsT, DEPTH, 4)
        bsT3 = v3(bsT, 4)
        kctx3 = v3(kctx, DEPTH)
        vctx4 = v4(vctx, DEPTH, 2)
        esink3 = v3(esink, DEPTH)
        gnw4 = v4(gnw, DEPTH, 2)
        pscale3 = v3(pscale, DEPTH)
        convT4 = v4(convT, DEPTH, 4)
        lgrow4 = v4(lgrow, DEPTH, 2)
        lgcol4 = v4(lgcol, DEPTH, 2)
        kdec3 = v3(kdec, 2)
        cdec3 = v3(cdec, 2)
        DfT3 = v3(DfT, 4)
        DbT3 = v3(DbT, 4)
        qdecF3 = v3(qdecF, 2)
        qdecB3 = v3(qdecB, 2)
        SrunF3 = v3(SrunF, 2)
        SrunB3 = v3(SrunB, 2)
        S03 = v3(S0, 2)
        pw4 = v4(poolW, DEPTH, 2)

        for g in groups:
            S.barrier()
            d_x = d_xs if g.sample else d_xp
            o_y = o_ys if g.sample else o_yp
            for t in range(g.ntile):
                for kc in range(KC):
                    DMA(X[:, kc, t * 512:(t + 1) * 512], d_x[kc * 128:(kc + 1) * 128, t * 512:(t + 1) * 512], (), [xk(kc, t)])
                    ACT(X[:, kc, t * 512:(t + 1) * 512], X[:, kc, t * 512:(t + 1) * 512], AF.Identity, [xk(kc, t)], [xk(kc, t)], scale=ALPHA)

            ckpt("xload")
            for l in range(DEPTH):
                last = (l == DEPTH - 1)
                cb_, cbk = fsc()
                cb2_, cb2k = fsc()
                DMA(cb_, d_cbig[:, 0:512], (), [cbk])
                DMA(cb2_[:, 0:256], d_cbig[:, 512:768], (), [cb2k])
                DMA(nwb, d_nwb[l:l + 1, :].partition_broadcast(128), (), ["nwb"])
                DMA(bsT[0:64, :], d_bsT[:, l * 512:(l + 1) * 512], (), ["bsT"])
                for h4 in range(4):
                    ACT(DfT3[:, h4, :], cb_[:, 0:128], AF.Exp, [cbk, "lgrow"], ["DfT"], scale=lgrow4[:, l, 0, h4:h4 + 1])
                    ACT(DbT3[:, h4, :], cb_[:, 128:256], AF.Exp, [cbk, "lgrow"], ["DbT"], scale=lgrow4[:, l, 1, h4:h4 + 1])
                TT(DfT3, DfT3, cb_[:, 256:384].unsqueeze(1).to_broadcast([128, 4, 128]), ALU.mult, ["DfT", cbk], ["DfT"])
                TT(DbT3, DbT3, cb_[:, 384:512].unsqueeze(1).to_broadcast([128, 4, 128]), ALU.mult, ["DbT", cbk], ["DbT"])
                for pr in range(2):
                    ACT(qdecF3[:, pr, :], cb2_[:, 0:128], AF.Exp, [cb2k, "lgcol"], ["qdecF"], scale=lgcol4[:, l, 0, pr:pr + 1])
                    ACT(qdecB3[:, pr, :], cb2_[:, 128:256], AF.Exp, [cb2k, "lgcol"], ["qdecB"], scale=lgcol4[:, l, 1, pr:pr + 1])
                ACT(kdec3[:, 0, :], lgrow4[:, l, 0, :], AF.Exp, ["lgrow", "csm"], ["kdec"], scale=col127)
                ACT(kdec3[:, 1, :], lgrow4[:, l, 1, :], AF.Exp, ["lgrow", "csm"], ["kdec"], scale=colj)
                TS(kdec, kdec, 0.125, None, ALU.mult, ALU.bypass, ["kdec"], ["kdec"])
                ACT(cdec3[:, 0, :], lgcol4[:, l, 0, :], AF.Exp, ["lgcol"], ["cdec"], scale=128.0)
                ACT(cdec3[:, 1, :], lgcol4[:, l, 1, :], AF.Exp, ["lgcol"], ["cdec"], scale=128.0)
                if g.sample:
                    VMEMSET(S0, 0.0, ["S0"])
                    S04 = v4(S0, 2, 2)
                    for dr in range(2):
                        DMA(stgS3, d_st0[l, dr].rearrange("(pr hh) d e -> (hh d) pr e", pr=2), (), ["stgS"])
                        for pr in range(2):
                            for hh in range(2):
                                TS(S04[hh * 64:(hh + 1) * 64, dr, pr, hh * 64:(hh + 1) * 64], stgS3[hh * 64:(hh + 1) * 64, pr, :], 1.0, None,
                                   ALU.mult, ALU.bypass, ["stgS"], ["S0"])

                ckpt("tables")
                VMEMSET(pd_g.rearrange("p a b -> p (a b)"), 0.0, ["pd"])
                order1 = list(reversed(range(g.ntile)))
                make_h(hA, l, g, order1[0], "a")
                if g.sample:
                    DMA(ropeC, d_ropeC[:, order1[0] * 512:(order1[0] + 1) * 512], (), ["ropeC"])
                    DMA(ropeS, d_ropeS[:, order1[0] * 512:(order1[0] + 1) * 512], (), ["ropeS"])
                pend = None
                for i1, t in enumerate(order1):
                    tcs = slice(t * 512, (t + 1) * 512)
                    hr = [hk(kc) for kc in range(KC)]
                    w, wk = wload(d_win[l][:, 0:256], KC, 256)
                    b1 = rr_all()
                    for kc in range(KC):
                        MM(PS(b1), w[:, kc, 0:128], hA[:, kc, :], kc == 0, kc == KC - 1, [wk, hk(kc)], [pk(b1)])
                    if g.sample:
                        b2 = rr_all()
                        for kc in range(KC):
                            MM(PS(b2), w[:, kc, 128:256], hA[:, kc, :], kc == 0, kc == KC - 1, [wk, hk(kc)], [pk(b2)])
                        f1, f1k = fsc()
                        f2, f2k = fsc()
                        TT(f1, PS(b1), ropeC, ALU.mult, [pk(b1), "ropeC"], [f1k])
                        TT(f2, PS(b2), ropeS, ALU.mult, [pk(b2), "ropeS"], [f2k])
                        TT(kaT_g[:, tcs], f1, f2, ALU.add, [f1k, f2k], ["kaT%d" % t])
                    else:
                        ACT(kaT_g[:, tcs], PS(b1), AF.Copy, [pk(b1)], ["kaT%d" % t])
                    if pend is not None:
                        next(pend, None)
                    ckpt("b0")
                    w, wk = wload(d_win[l][:, 256:512], KC, 256)
                    for c2 in range(2):
                        b = rr_all()
                        for kc in range(KC):
                            MM(PS(b), w[:, kc, c2 * 128:(c2 + 1) * 128], hA[:, kc, :], kc == 0, kc == KC - 1, [wk, hk(kc)], [pk(b)])
                        for (c0, n, sq, tau0) in g.segs(t):
                            base = sq * (16 + g.seqlen) + 8 + tau0
                            ACT(pd_g[:, c2, base:base + n], PS(b, c0, c0 + n), AF.Copy, [pk(b)], ["pd"])
                    if pend is not None:
                        next(pend, None)
                    ckpt("b1")
                    w, wk = wload(d_win[l][:, 512:768], KC, 256)
                    for c4 in range(4):
                        cg = t * 4 + c4
                        b = rr_all()
                        for kc in range(KC):
                            MM(PS(b, 0, 256), hA[:, kc, c4 * 128:(c4 + 1) * 128], w[:, kc, :], kc == 0, kc == KC - 1, [wk, hk(kc)], [pk(b)])
                        ACT(va_g[:, cg, :], PS(b, 0, 128), AF.Copy, [pk(b)], ["va%d" % cg])
                        import os
                        dbgm = os.environ.get("DBGM", "full")
                        if not g.sample and dbgm != "nocopy":
                            if dbgm == "actcopy":
                                ACT(stg, PS(b, 0, 256), AF.Copy, [pk(b)], ["stg"])
                            else:
                                VCP(stg, PS(b, 0, 256), [pk(b)], ["stg"])
                            if dbgm not in ("nodma", "actcopy"):
                                DMA(o_cv[l, cg * 128:(cg + 1) * 128, :], stg[:, 0:128], ["stg"], [])
                                DMA(o_ck[l, cg * 128:(cg + 1) * 128, :], stg[:, 128:256], ["stg"], [])
                    if pend is not None:
                        for _ in pend:
                            pass
                    ckpt("b2")
                    w, wk = wload(d_win[l][:, 768:1024], KC, 256)
                    for c4 in range(4):
                        b = rr_all()
                        for kc in range(KC):
                            MM(PS(b, 0, 256), hA[:, kc, c4 * 128:(c4 + 1) * 128], w[:, kc, :], kc == 0, kc == KC - 1, [wk, hk(kc)], [pk(b)])
                        for h4 in range(4):
                            ACT(kdb1[:, c4, h4 * 64:(h4 + 1) * 64], PS(b, h4 * 64, (h4 + 1) * 64), AF.Identity, [pk(b), "kdec"], ["p1a%d" % c4],
                                scale=kdec3[:, 1, h4:h4 + 1])
                    ckpt("b3")
                    w, wk = wload(d_win[l][:, 1024:1280], KC, 256)
                    for c4 in range(4):
                        b = rr_all()
                        for kc in range(KC):
                            MM(PS(b, 0, 256), hA[:, kc, c4 * 128:(c4 + 1) * 128], w[:, kc, :], kc == 0, kc == KC - 1, [wk, hk(kc)], [pk(b)])
                        ACT(vcb1[:, c4, :], PS(b, 0, 256), AF.Copy, [pk(b)], ["p1b%d" % c4])
                    if i1 + 1 < len(order1):
                        tn_ = order1[i1 + 1]
                        make_h(hA, l, g, tn_, "a")
                        if g.sample:
                            DMA(ropeC, d_ropeC[:, tn_ * 512:(tn_ + 1) * 512], (), ["ropeC"])
                            DMA(ropeS, d_ropeS[:, tn_ * 512:(tn_ + 1) * 512], (), ["ropeS"])
                    def scan_gen(t=t):
                        for c4 in reversed(range(4)):
                            cg = t * 4 + c4
                            sq = cg // g.cps
                            if cg % g.cps == g.cps - 1:
                                if g.sample:
                                    VCP(SrunB, S0[:, 256:512], ["S0"], ["SrunB"])
                                else:
                                    VMEMSET(SrunB, 0.0, ["SrunB"])
                            ACT(Sb_g[:, cg, :], SrunB, AF.Copy, ["SrunB"], ["Sb%d" % cg])
                            b = rr_all()
                            for pr in range(2):
                                MM(PS(b, pr * 128, (pr + 1) * 128), kdb1[:, c4, pr * 128:(pr + 1) * 128], vcb1[:, c4, pr * 128:(pr + 1) * 128],
                                   True, True, ["p1a%d" % c4, "p1b%d" % c4], [pk(b)])
                            f, fk = fsc()
                            TT(f[:, 0:256], PS(b, 0, 256), bd2, ALU.mult, [pk(b), "bd2"], [fk])
                            for pr in range(2):
                                STT(SrunB3[:, pr, :], SrunB3[:, pr, :], cdec3[:, 1, pr:pr + 1], f[:, pr * 128:(pr + 1) * 128], ALU.mult, ALU.add,
                                    ["SrunB", "cdec", fk], ["SrunB"])
                            if (not g.sample) and cg % g.cps == 0:
                                state_out(SrunB3, "SrunB", l, sq, 1)
                            yield

                    pend = scan_gen()
                if pend is not None:
                    for _ in pend:
                        pass
                ckpt("p1_%s%d" % (g.name, l))
                VMEMSET(vz, 0.0, ["vz%d" % c for c in range(4)])
                for t in range(g.ntile):
                    tcs = slice(t * 512, (t + 1) * 512)
                    if t == 0:
                        make_h(hA, l, g, t, "a")
                        if g.sample:
                            DMA(ropeC, d_ropeC[:, tcs], (), ["ropeC"])
                            DMA(ropeS, d_ropeS[:, tcs], (), ["ropeS"])
                    w, wk = wload(d_win[l][:, 1280:1536], KC, 256)
                    if g.sample:
                        wp, wpk = wload(d_win[l][:, 1536:1792], KC, 256)
                    for gq in range(2):
                        b1 = rr_all()
                        for kc in range(KC):
                            MM(PS(b1), w[:, kc, gq * 128:(gq + 1) * 128], hA[:, kc, :], kc == 0, kc == KC - 1, [wk, hk(kc)], [pk(b1)])
                        if g.sample:
                            b2 = rr_all()
                            for kc in range(KC):
                                MM(PS(b2), wp[:, kc, gq * 128:(gq + 1) * 128], hA[:, kc, :], kc == 0, kc == KC - 1, [wpk, hk(kc)], [pk(b2)])
                            f1, f1k = fsc()
                            f2, f2k = fsc()
                            TT(f1, PS(b1), ropeC, ALU.mult, [pk(b1), "ropeC"], [f1k])
                            TT(f2, PS(b2), ropeS, ALU.mult, [pk(b2), "ropeS"], [f2k])
                            TT(qaT[:, gq, :], f1, f2, ALU.add, [f1k, f2k], ["qaT"])
                        else:
                            ACT(qaT[:, gq, :], PS(b1), AF.Copy, [pk(b1)], ["qaT"])
                    for (c0, n, sq, tau0) in g.segs(t):
                        base = sq * (16 + g.seqlen) + tau0
                        for c2 in range(2):
                            A_, Ak = fsc()
                            B_, Bk = fsc()
                            M_, Mk = fsc()
                            hseg = [(0, n)] if n + 16 <= 512 else [(0, n // 2), (n // 2, n - n // 2)]
                            for (o0, nn) in hseg:
                                Pp = pd_g[:, c2, base + o0:base + o0 + nn + 16]
                                W_ = nn + 16
                                TT(A_[:, 1:W_], Pp[:, 0:W_ - 1], Pp[:, 1:W_], ALU.add, ["pd"], [Ak])
                                TT(B_[:, 2:W_ - 1], A_[:, 1:W_ - 2], A_[:, 3:W_], ALU.add, [Ak], [Bk])
                                if c2 == 0:
                                    TS(M_[0:64, 0:nn], A_[0:64, 8:8 + nn], invw[0:64, 0:1], None, ALU.mult, ALU.bypass, [Ak, "csm"], [Mk])
                                    TS(M_[64:128, 0:nn], B_[64:128, 8:8 + nn], invw[64:128, 0:1], None, ALU.mult, ALU.bypass, [Bk, "csm"], [Mk])
                                else:
                                    TT(A_[:, 4:W_ - 3], B_[:, 2:W_ - 5], B_[:, 6:W_ - 1], ALU.add, [Bk, Ak], [Ak])
                                    TT(B_[64:128, 8:W_ - 8], A_[64:128, 4:W_ - 12], A_[64:128, 12:W_ - 4], ALU.add, [Ak, Bk], [Bk])
                                    TS(M_[0:64, 0:nn], A_[0:64, 8:8 + nn], invw[0:64, 1:2], None, ALU.mult, ALU.bypass, [Ak, "csm"], [Mk])
                                    TS(M_[64:128, 0:nn], B_[64:128, 8:8 + nn], invw[64:128, 1:2], None, ALU.mult, ALU.bypass, [Bk, "csm"], [Mk])
                                if tau0 + o0 == 0:
                                    TT(M_[:, 0:8], M_[:, 0:8], corrS[:, c2, :], ALU.mult, [Mk, "csm"], [Mk])
                                if tau0 + o0 + nn == g.seqlen:
                                    TT(M_[:, nn - 8:nn], M_[:, nn - 8:nn], corrE[:, c2, :], ALU.mult, [Mk, "csm"], [Mk])
                                df, dfk = bsc()
                                TT(df[:, 0:nn], M_[:, 0:nn], Pp[:, 8:8 + nn], ALU.subtract, [Mk, "pd"], [dfk])
                                b = rr_all()
                                MM(PS(b, 0, nn), pw4[:, l, c2, :], df[:, 0:nn], True, True, ["poolW", dfk], [pk(b)])
                                ACT(yD[:, c2, c0 + o0:c0 + o0 + nn], PS(b, 0, nn), AF.Identity, [pk(b), "pscale"], ["p1b%d" % (2 * c2), "p1b%d" % (2 * c2 + 1)],
                                    scale=pscale3[:, l, c2:c2 + 1])

                    def proj_gen():
                        w, wk = wload(d_win[l][:, 1792:2048], KC, 256)
                        for h4 in range(4):
                            b = pbank()
                            for kc in range(KC):
                                MM(PS(b)[0:64, :], w[:, kc, h4 * 64:(h4 + 1) * 64], hA[:, kc, :], kc == 0, kc == KC - 1, [wk, hk(kc)], [pk(b)])
                            ACT(ubT[0:64, h4, :], PS(b)[0:64, :], AF.Copy, [pk(b)], ["ubT"])
                        yield
                        w, wk = wload(d_win[l][:, 2048:2304], KC, 256)
                        for pr in range(2):
                            b = pbank()
                            for kc in range(KC):
                                MM(PS(b), w[:, kc, pr * 128:(pr + 1) * 128], hA[:, kc, :], kc == 0, kc == KC - 1, [wk, hk(kc)], [pk(b)])
                            ACT(qcT[:, pr, :], PS(b), AF.Copy, [pk(b)], ["qcT"])
                            TT(v3(qdf[:, pr, :], 4), v3(PS(b), 4), qdecF3[:, pr, :].unsqueeze(1).to_broadcast([128, 4, 128]), ALU.mult,
                               [pk(b), "qdecF"], ["qdf"])
                            TT(v3(qdb[:, pr, :], 4), v3(PS(b), 4), qdecB3[:, pr, :].unsqueeze(1).to_broadcast([128, 4, 128]), ALU.mult,
                               [pk(b), "qdecB"], ["qdb"])
                        yield
                        w, wk = wload(d_win[l][:, 2304:2560], KC, 256)
                        for pr in range(2):
                            b = pbank()
                            for kc in range(KC):
                                MM(PS(b), w[:, kc, pr * 128:(pr + 1) * 128], hA[:, kc, :], kc == 0, kc == KC - 1, [wk, hk(kc)], [pk(b)])
                            ACT(kcT[:, pr, :], PS(b), AF.Copy, [pk(b)], ["kcT"])
                        yield
                        for (c0, dstg, key) in ((2560, sgf, "sgf"), (2816, sgb, "sgb")):
                            w, wk = wload(d_win[l][:, c0:c0 + 256], KC, 256)
                            for pr in range(2):
                                b = pbank()
                                for kc in range(KC):
                                    MM(PS(b), w[:, kc, pr * 128:(pr + 1) * 128], hA[:, kc, :], kc == 0, kc == KC - 1, [wk, hk(kc)], [pk(b)])
                                ACT(dstg[:, pr, :], PS(b), AF.Silu, [pk(b)], [key])
                            yield
                        w, wk = wload(d_win[l][:, 3072:3328], KC, 256)
                        for c4 in range(4):
                            b = pbank()
                            for kc in range(KC):
                                MM(PS(b, 0, 256), hA[:, kc, c4 * 128:(c4 + 1) * 128], w[:, kc, :], kc == 0, kc == KC - 1, [wk, hk(kc)], [pk(b)])
                            f, fk = fsc()
                            S.add("dve", lambda h, o=f[:, 256:262], i=PS(b, 0, 256): h.bn_stats(out=o, in_=i), [pk(b)], [fk])
                            S.add("dve", lambda h, o=f[:, 262:264], i=f[:, 256:262]: h.bn_aggr(out=o, in_=i), [fk], [fk])
                            ACT(f[:, 264:265], f[:, 263:264], AF.Ln, [fk], [fk], bias=EPS)
                            ACT(f[:, 264:265], f[:, 264:265], AF.Exp, [fk], [fk], scale=-0.5)
                            TS(f[:, 0:256], PS(b, 0, 256), f[:, 262:263], f[:, 264:265], ALU.subtract, ALU.mult, [pk(b), fk], [fk])
                            TT(f[:, 0:256], f[:, 0:256], nwb[:, 0:256], ALU.mult, [fk, "nwb"], [fk])
                            TT(vbn[:, c4, :], f[:, 0:256], nwb[:, 256:512], ALU.add, [fk, "nwb"], ["vbn%d" % c4])
                        yield
                        w, wk = wload(d_win[l][:, 3328:3584], KC, 256)
                        for c4 in range(4):
                            b = pbank()
                            for kc in range(KC):
                                MM(PS(b, 0, 256), hA[:, kc, c4 * 128:(c4 + 1) * 128], w[:, kc, :], kc == 0, kc == KC - 1, [wk, hk(kc)], [pk(b)])
                            for h4 in range(4):
                                ACT(kdf[:, c4, h4 * 64:(h4 + 1) * 64], PS(b, h4 * 64, (h4 + 1) * 64), AF.Identity, [pk(b), "kdec"], ["kdf%d" % c4],
                                    scale=kdec3[:, 0, h4:h4 + 1])
                        yield
                        w, wk = wload(d_win[l][:, 3584:3840], KC, 256)
                        for c4 in range(4):
                            b = pbank()
                            for kc in range(KC):
                                MM(PS(b, 0, 256), hA[:, kc, c4 * 128:(c4 + 1) * 128], w[:, kc, :], kc == 0, kc == KC - 1, [wk, hk(kc)], [pk(b)])
                            ACT(vcb[:, c4, :], PS(b, 0, 256), AF.Copy, [pk(b)], ["vcb%d" % c4])
                            psv = PS(b, 0, 256).rearrange("p (a h e) -> p a h e", a=2, h=2)
                            for hh in range(2):
                                ACT(vz5[:, c4, :, hh, hh * 64:(hh + 1) * 64], psv[:, :, hh, :], AF.Copy, [pk(b)], ["vz%d" % c4])
                        yield

                    def att_keys(qi):
                        cq = t * 4 + qi
                        if g.sample:
                            kl = []
                            if cq - 1 >= 0:
                                kl.append(("loc", cq - 1, mprev, "cbf"))
                            kl.append(("loc", cq, None, None))
                            if cq + 1 < g.nchunk:
                                kl.append(("loc", cq + 1, mnext, "cbf"))
                            kl += [("ctx", 0, None, None), ("ctx", 1, None, None)]
                        else:
                            sq = cq // g.cps
                            kl = [("loc", sq * g.cps + j, None, None) for j in range(g.cps)]
                        return kl

                    def att_scores(qi, kvh):
                        rows = slice(kvh * 64, (kvh + 1) * 64)
                        out = []
                        for (kind, ck_, msk, mk) in att_keys(qi):
                            bS = rr_st()
                            if kind == "loc":
                                kT = kaT_g[rows, ck_ * 128:(ck_ + 1) * 128]
                                kkey = "kaT%d" % (ck_ // 4)
                                vv = va_g[:, ck_, kvh * 64:(kvh + 1) * 64]
                                vkey = "va%d" % ck_
                            else:
                                kT = kctx3[rows, l, ck_ * 128:(ck_ + 1) * 128]
                                kkey = "kctx"
                                vv = vctx4[:, l, ck_, kvh * 64:(kvh + 1) * 64]
                                vkey = "vctx"
                            MM(PS(bS, 0, 256), kT, qaT[rows, :, qi * 128:(qi + 1) * 128], True, True, [kkey, "qaT"], [pk(bS)])
                            ei = er_i[0] % NER
                            er_i[0] += 1
                            E, Ek = ering[ei], "er%d" % ei
                            ACT(E, PS(bS, 0, 256), AF.Exp, [pk(bS)], [Ek], scale=0.125)
                            if msk is not None:
                                TT(E, E, msk, ALU.mult, [Ek, mk], [Ek])
                            out.append((E, Ek, vv, vkey))
                        return out

                    def att_out(qi, kvh, it, es_):
                        bN, bD = 5, 6
                        for n, (E, Ek, vv, vkey) in enumerate(es_):
                            MM(PS(bN, 0, 256)[0:64, :], vv, E, n == 0, n == len(es_) - 1, [vkey, Ek], [pk(bN)])
                            MM(PS(bD, 0, 256)[0:64, :], ones64, E, n == 0, n == len(es_) - 1, ["cbf", Ek], [pk(bD)])
                        f, fk = fsc()
                        for gq in range(2):
                            h4 = kvh * 2 + gq
                            ACT(f[0:64, gq * 128:(gq + 1) * 128], PS(bD, gq * 128, (gq + 1) * 128)[0:64, :], AF.Ln, [pk(bD), "esink"], [fk],
                                bias=esink3[0:64, l, h4:h4 + 1])
                        ACT(f[0:64, 0:256], f[0:64, 0:256], AF.Exp, [fk], [fk], scale=-1.0)
                        for gq in range(2):
                            TT(yA[gq * 64:(gq + 1) * 64, kvh, qi * 128:(qi + 1) * 128], PS(bN, gq * 128, (gq + 1) * 128)[0:64, :],
                               f[0:64, gq * 128:(gq + 1) * 128], ALU.mult, [pk(bN), fk], ["p1a%d" % (2 * kvh), "p1a%d" % (2 * kvh + 1)])

                    pbank = RR([0, 1, 2, 7])
                    rr_st = RR([3, 4])
                    gen = proj_gen()
                    its = [(qi, kvh) for qi in range(4) for kvh in range(2)]
                    nxt = att_scores(*its[0])
                    for it, (qi, kvh) in enumerate(its):
                        next(gen, None)
                        cur = nxt
                        if it + 1 < len(its):
                            nxt = att_scores(*its[it + 1])
                        att_out(qi, kvh, it, cur)
                    for _ in gen:
                        pass
                    ckpt("proj")

                    ckpt("attn")
                    for h4 in range(4):
                        b = rr_all()
                        for c4 in range(4):
                            MM(PS(b, c4 * 128, (c4 + 1) * 128)[0:64, :], vbn[:, c4, h4 * 64:(h4 + 1) * 64], wsT4[:, l, h4, :], True, True,
                               ["vbn%d" % c4, "wsT"], [pk(b)])
                        f, fk = fsc()
                        TT(v3(f[0:64, :], 4), v3(PS(b)[0:64, :], 4), bsT3[0:64, h4, :].unsqueeze(1).to_broadcast([64, 4, 128]), ALU.add,
                           [pk(b), "bsT"], [fk])
                        pr, hh = h4 // 2, h4 % 2
                        TT(yT[hh * 64:(hh + 1) * 64, 2 + pr, :], f[0:64, :], ubT[0:64, h4, :], ALU.mult, [fk, "ubT"], [hk(2 + pr)])

                    ckpt("sgu")
                    bO = {(0, 0): 4, (0, 1): 5, (1, 0): 6, (1, 1): 7}
                    for c4 in range(4):
                        cg = t * 4 + c4
                        sq = cg // g.cps
                        if cg % g.cps == 0:
                            if g.sample:
                                VCP(SrunF, S0[:, 0:256], ["S0"], ["SrunF"])
                            else:
                                VMEMSET(SrunF, 0.0, ["SrunF"])
                        ACT(sfr[c4], SrunF, AF.Copy, ["SrunF"], ["sfr%d" % c4])
                        bC = rr_lo()
                        for pr in range(2):
                            MM(PS(bC, pr * 128, (pr + 1) * 128), kdf[:, c4, pr * 128:(pr + 1) * 128], vcb[:, c4, pr * 128:(pr + 1) * 128],
                               True, True, ["kdf%d" % c4, "vcb%d" % c4], [pk(bC)])
                        f, fk = fsc()
                        TT(f[:, 0:256], PS(bC, 0, 256), bd2, ALU.mult, [pk(bC), "bd2"], [fk])
                        for pr in range(2):
                            STT(SrunF3[:, pr, :], SrunF3[:, pr, :], cdec3[:, 0, pr:pr + 1], f[:, pr * 128:(pr + 1) * 128], ALU.mult, ALU.add,
                                ["SrunF", "cdec", fk], ["SrunF"])
                        if (not g.sample) and cg % g.cps == g.cps - 1:
                            state_out(SrunF3, "SrunF", l, sq, 0)

                    def ret_scores(c4):
                        cc = slice(c4 * 128, (c4 + 1) * 128)
                        bSh = [rr_lo(), rr_lo()]
                        for h4 in range(4):
                            pr, hh = h4 // 2, h4 % 2
                            rws = slice(hh * 64, (hh + 1) * 64)
                            MM(PS(bSh[hh], pr * 128, (pr + 1) * 128), kcT[rws, pr, cc], qcT[rws, pr, cc], True, True, ["kcT", "qcT"], [pk(bSh[hh])])
                        sf, sb_ = scf[c4 % 2], scb[c4 % 2]
                        sfk, sbk = "scf%d" % (c4 % 2), "scb%d" % (c4 % 2)
                        sf4 = sf.rearrange("p (a h i) -> p a h i", a=2, h=2)
                        sb4 = sb_.rearrange("p (a h i) -> p a h i", a=2, h=2)
                        Df4 = DfT.rearrange("p (a h i) -> p a h i", a=2, h=2)
                        Db4 = DbT.rearrange("p (a h i) -> p a h i", a=2, h=2)
                        for hh in range(2):
                            TT(sf4[:, :, hh, :], v3(PS(bSh[hh], 0, 256), 2), Df4[:, :, hh, :], ALU.mult, [pk(bSh[hh]), "DfT"], [sfk])
                            TT(sb4[:, :, hh, :], v3(PS(bSh[hh], 0, 256), 2), Db4[:, :, hh, :], ALU.mult, [pk(bSh[hh]), "DbT"], [sbk])

                    def ret_out(c4):
                        cg = t * 4 + c4
                        cc = slice(c4 * 128, (c4 + 1) * 128)
                        sf, sb_ = scf[c4 % 2], scb[c4 % 2]
                        sfk, sbk = "scf%d" % (c4 % 2), "scb%d" % (c4 % 2)
                        for dr in range(2):
                            sc_, sck = (sf, sfk) if dr == 0 else (sb_, sbk)
                            qd = qdf if dr == 0 else qdb
                            qdk = "qdf" if dr == 0 else "qdb"
                            for pr in range(2):
                                b = bO[(dr, pr)]
                                for hh in range(2):
                                    h4 = pr * 2 + hh
                                    MM(PS(b, c4 * 128, (c4 + 1) * 128), vz5[:, c4, pr, hh, :], sc_[:, h4 * 128:(h4 + 1) * 128], hh == 0, False,
                                       ["vz%d" % c4, sck], [pk(b)])
                                if dr == 0:
                                    st_ap, stk = sfr[c4][:, pr * 128:(pr + 1) * 128], "sfr%d" % c4
                                else:
                                    st_ap, stk = Sb_g[:, cg, pr * 128:(pr + 1) * 128], "Sb%d" % cg
                                MM(PS(b, c4 * 128, (c4 + 1) * 128), st_ap, qd[:, pr, cc], False, True, [stk, qdk], [pk(b)])

                    ret_scores(0)
                    for c4 in range(4):
                        if c4 + 1 < 4:
                            ret_scores(c4 + 1)
                        ret_out(c4)
                    for pr in range(2):
                        ydir = []
                        for dr in range(2):
                            b = bO[(dr, pr)]
                            o32, o32k = fsc()
                            ob, obk = bsc()
                            oq, oqk = bsc()
                            ACT(o32, PS(b), AF.Copy, [pk(b)], [o32k])
                            ACT(ob, PS(b), AF.Copy, [pk(b)], [obk])
                            ACT(oq, PS(b), AF.Square, [pk(b)], [oqk])
                            bM, bQ = rr_lo(), rr_lo()
                            MM(PS(bM), bd64, ob, True, True, ["cbf", obk], [pk(bM)])
                            MM(PS(bQ), bd64, oq, True, True, ["cbf", oqk], [pk(bQ)])
                            fv, fvk = fsc()
                            ACT(fv, PS(bM), AF.Square, [pk(bM)], [fvk])
                            TT(fv, PS(bQ), fv, ALU.subtract, [pk(bQ), fvk], [fvk])
                            ACT(fv, fv, AF.Ln, [fvk], [fvk], bias=EPS)
                            ACT(fv, fv, AF.Exp, [fvk], [fvk], scale=-0.5)
                            TT(o32, o32, PS(bM), ALU.subtract, [o32k, pk(bM)], [o32k])
                            TT(o32, o32, fv, ALU.mult, [o32k, fvk], [o32k])
                            sg = sgf if dr == 0 else sgb
                            STT(o32, o32, gnw4[:, l, dr, pr:pr + 1], sg[:, pr, :], ALU.mult, ALU.mult, [o32k, "gnw", "sgf" if dr == 0 else "sgb"], [o32k])
                            ydir.append((o32, o32k))
                        TT(yT[:, 4 + pr, :], ydir[0][0], ydir[1][0], ALU.add, [ydir[0][1], ydir[1][1]], [hk(4 + pr)])

                    ckpt("ret")
                    ckpt("poolmix")
                    ga = mvec(l, g.ci, 2)
                    rr6 = RR(range(6))
                    for blk in range(4):
                        w, wk = wload(d_wout[l][:, blk * 256:(blk + 1) * 256], KC, 256)
                        for j in range(2):
                            m = blk * 2 + j
                            b = rr6()
                            for kc in range(KC):
                                if kc < 2:
                                    rhs_, rk_ = yA[:, kc, :], ["p1a%d" % (2 * kc), "p1a%d" % (2 * kc + 1)]
                                elif kc >= 6:
                                    rhs_, rk_ = yD[:, kc - 6, :], ["p1b%d" % (2 * (kc - 6)), "p1b%d" % (2 * (kc - 6) + 1)]
                                else:
                                    rhs_, rk_ = yT[:, kc, :], [hk(kc)]
                                MM(PS(b), w[:, kc, j * 128:(j + 1) * 128], rhs_, kc == 0, kc == KC - 1, [wk] + rk_, [pk(b)])
                            STT(X[:, m, tcs], PS(b), ga[:, m:m + 1], X[:, m, tcs], ALU.mult, ALU.add, [pk(b), "modv%d" % l, xk(m, t)], [xk(m, t)])
                            if m >= 2:
                                ln_stats(t, m - 2, 6, 7)
                    ln_stats(t, KC - 2, 6, 7)
                    ln_stats(t, KC - 1, 6, 7)
                    if t + 1 < g.ntile:
                        make_h(hA, l, g, t + 1, "a")
                        if g.sample:
                            tn = slice((t + 1) * 512, (t + 2) * 512)
                            DMA(ropeC, d_ropeC[:, tn], (), ["ropeC"])
                            DMA(ropeS, d_ropeS[:, tn], (), ["ropeS"])
                    ln_finish(t, lvec3[:, l, 0:8], lvec3[:, l, 8:16], 6, 7)

                ckpt("s1_%s%d" % (g.name, l))
                if (not g.sample) and l == 0:
                    for l2 in range(1, DEPTH):
                        do_mod(l2)
                S.barrier()
                T = 2 if g.sample else 1
                hcols = []
                if g.sample:
                    for gi in range(g.ntile // T):
                        if (gi + 1) * T * 512 < g.nt:
                            hcols.append((gi + 1) * T * 512)
                    sv = dvec4[:, l, g.ci, KC:2 * KC]
                    bv = mvec(l, g.ci, 3)
                    for hi, col in enumerate(hcols):
                        for kc in range(KC):
                            ACT(hFh[:, kc, hi:hi + 1], X[:, kc, col:col + 1], AF.Identity, [xk(kc, col // 512), "dvec%d" % l, "modv%d" % l], ["hFh"],
                                scale=sv[:, kc:kc + 1], bias=bv[:, kc:kc + 1])
                for gi in range(g.ntile // T):
                    tiles = [gi * T + j for j in range(T)]
                    gc0 = tiles[0] * 512
                    W_ = T * 512
                    hkeys = ["hF%d" % kc for kc in range(KC)]
                    if gi == 0:
                        for j, t in enumerate(tiles):
                            make_h(hF[:, :, j * 512:(j + 1) * 512], l, g, t, "f", dkeys=hkeys)
                    halo = []
                    if g.sample:
                        if gc0 + W_ < g.nt:
                            halo.append(("R", hcols.index(gc0 + W_)))
                    pf = [wload(d_up[l][:, i_ * 256:(i_ + 1) * 256], KC, 256) for i_ in range(min(NSLOT, NFC))]
                    for i in range(NFC):
                        w, wk = pf.pop(0)
                        pb = {}
                        for j in range(T):
                            for ag in range(2):
                                b = rr_all()
                                pb[(j, ag)] = b
                                for kc in range(KC):
                                    MM(PS(b), w[:, kc, ag * 128:(ag + 1) * 128], hF[:, kc, j * 512:(j + 1) * 512], kc == 0, kc == KC - 1,
                                       [wk, hkeys[kc]], [pk(b)])
                        bH = None
                        if halo:
                            bH = rr_all()
                            for ag in range(2):
                                for k_, (s_, hx) in enumerate(halo):
                                    for kc in range(KC):
                                        MM(PS(bH, ag * 2 + k_, ag * 2 + k_ + 1), w[:, kc, ag * 128:(ag + 1) * 128], hFh[:, kc, hx:hx + 1],
                                           kc == 0, kc == KC - 1, [wk, "hFh"], [pk(bH)])
                        hl = None
                        if g.sample and gc0 > 0:
                            hl, hlk = fsc()
                            ACT(hl[:, 0:2], hsave[:, 2 * i:2 * i + 2], AF.Copy, ["hsave%d" % i], [hlk])
                        if g.sample and gc0 + W_ < g.nt:
                            for ag in range(2):
                                bl = pb[(T - 1, ag)]
                                ACT(hsave[:, 2 * i + ag:2 * i + ag + 1], PS(bl, 511, 512), AF.Copy, [pk(bl)], ["hsave%d" % i])
                        if i + NSLOT < NFC:
                            pf.append(wload(d_up[l][:, (i + NSLOT) * 256:(i + NSLOT + 1) * 256], KC, 256))
                        accs = {}
                        for j in range(T):
                            t = tiles[j]
                            for ag in range(2):
                                ch = ag * NFC + i
                                cw0 = convT4[:, l, 0, ch:ch + 1]
                                cw1 = convT4[:, l, 1, ch:ch + 1]
                                cw2 = convT4[:, l, 2, ch:ch + 1]
                                cbv = convT4[:, l, 3, ch:ch + 1]
                                b = pb[(j, ag)]
                                acc, acck = fsc()
                                accs[(j, ag)] = (acc, acck)
                                ACT(acc, PS(b), AF.Identity, [pk(b), "convT"], [acck], scale=cw1, bias=cbv)
                                for (c0, n, sq, tau0) in g.segs(t):
                                    STT(acc[:, c0 + 1:c0 + n], PS(b, c0, c0 + n - 1), cw0, acc[:, c0 + 1:c0 + n], ALU.mult, ALU.add,
                                        [pk(b), "convT", acck], [acck])
                                    STT(acc[:, c0:c0 + n - 1], PS(b, c0 + 1, c0 + n), cw2, acc[:, c0:c0 + n - 1], ALU.mult, ALU.add,
                                        [pk(b), "convT", acck], [acck])
                                if g.sample:
                                    if j > 0:
                                        bp = pb[(j - 1, ag)]
                                        STT(acc[:, 0:1], PS(bp, 511, 512), cw0, acc[:, 0:1], ALU.mult, ALU.add, [pk(bp), "convT", acck], [acck])
                                    elif gc0 > 0:
                                        STT(acc[:, 0:1], hl[:, ag:ag + 1], cw0, acc[:, 0:1], ALU.mult, ALU.add,
                                            [hlk, "convT", acck], [acck])
                                    if j < T - 1:
                                        bn_ = pb[(j + 1, ag)]
                                        STT(acc[:, 511:512], PS(bn_, 0, 1), cw2, acc[:, 511:512], ALU.mult, ALU.add, [pk(bn_), "convT", acck], [acck])
                                    elif gc0 + W_ < g.nt:
                                        hi = [k for k, (s_, _) in enumerate(halo) if s_ == "R"][0]
                                        STT(acc[:, 511:512], PS(bH, ag * 2 + hi, ag * 2 + hi + 1), cw2, acc[:, 511:512], ALU.mult, ALU.add,
                                            [pk(bH), "convT", acck], [acck])
                            aa, aak = accs[(j, 0)]
                            gg_, ggk = accs[(j, 1)]
                            ACT(aa, aa, AF.Silu, [aak], [aak])
                            S.add("pool", lambda h, o=hid[:, i, j * 512:(j + 1) * 512], a_=aa, b_=gg_: h.tensor_tensor(out=o, in0=a_, in1=b_, op=ALU.mult),
                                  [aak, ggk], ["hid%d" % i])
                    gf = mvec(l, g.ci, 5)
                    rr4 = RR(range(4))
                    prev = None
                    for m in range(KC):
                        w, wk = wload(d_down[l][:, m * 128:(m + 1) * 128], NFC, 128)
                        for j, t in enumerate(tiles):
                            b = rr4()
                            for k in range(NFC):
                                MM(PS(b), w[:, k, :], hid[:, k, j * 512:(j + 1) * 512], k == 0, k == NFC - 1, [wk, "hid%d" % k], [pk(b)])
                            tcs = slice(t * 512, (t + 1) * 512)
                            STT(X[:, m, tcs], PS(b), gf[:, m:m + 1], X[:, m, tcs], ALU.mult, ALU.add, [pk(b), "modv%d" % l, xk(m, t)], [xk(m, t)])
                        if prev is not None:
                            for j, t in enumerate(tiles):
                                ln_stats(t, prev, 4 + 2 * j, 5 + 2 * j)
                        prev = m
                    for j, t in enumerate(tiles):
                        ln_stats(t, prev, 4 + 2 * j, 5 + 2 * j)
                    if gi + 1 < g.ntile // T:
                        for j2 in range(T):
                            make_h(hF[:, :, j2 * 512:(j2 + 1) * 512], l, g, (gi + 1) * T + j2, "f", dkeys=hkeys)
                    for j, t in enumerate(tiles):
                        ln_finish(t, lvec3[:, l, 16:24], lvec3[:, l, 24:32], 4 + 2 * j, 5 + 2 * j)
                        if last:
                            for kc in range(KC):
                                DMA(o_y[kc * 128:(kc + 1) * 128, t * 512:(t + 1) * 512], X[:, kc, t * 512:(t + 1) * 512], [xk(kc, t)], [])
                S.barrier()
    except _Stop:
        pass

    S.emit(nc, block, semobj)
    es.close()
    return nc


_CACHE = {}


def _prep_inputs(inp):
    f32 = lambda a: np.ascontiguousarray(np.asarray(a, dtype=np.float32))
    x_prompt = f32(inp["x_prompt"])
    x_sample = f32(inp["x_sample"])
    ck = f32(inp["cache_attn_k"])
    cv = f32(inp["cache_attn_v"])
    st = f32(inp["state_ret"])
    c = f32(inp["c"])
    c_ctx = f32(inp["c_ctx"])
    w_in = f32(inp["w_in"])
    big, small, bd2, cb = _const_tables()
    ropeC, ropeS = _rope_tables()
    shared = {
        "w_ada": f32(inp["w_ada"]),
        "b_adaT": np.ascontiguousarray(np.concatenate([_fm(f32(inp["b_ada"])[l], 48) for l in range(DEPTH)], axis=1)),
        "w_in_ext": np.ascontiguousarray(w_in[:, :, _win_ext_cols()]),
        "w_out": f32(inp["w_out"]),
        "ffn_up_ext": np.ascontiguousarray(f32(inp["ffn_up"])[:, :, _up_ext_cols()]),
        "ffn_down": f32(inp["ffn_down"]),
        "c_big": big, "c_small": small, "c_bd2": bd2, "c_bf": cb, "ropeC": ropeC, "ropeS": ropeS,
    }
    cw = f32(inp["ffn_conv_w"])
    cbias = f32(inp["ffn_conv_b"])
    conv = np.zeros((128, DEPTH, 4, 44), np.float32)
    for l in range(DEPTH):
        for j in range(3):
            conv[:, l, j, :] = _fm(cw[l, j], 44)
        conv[:, l, 3, :] = _fm(cbias[l], 44)
    shared["convT"] = conv.reshape(128, -1)
    lnw, lnb = f32(inp["ln_w"]), f32(inp["ln_b"])
    ln = np.zeros((128, DEPTH, 2, 2, KC), np.float32)
    for l in range(DEPTH):
        for i in range(2):
            ln[:, l, i, 0, :] = _fm(lnw[l, i], KC)
            ln[:, l, i, 1, :] = _fm(lnb[l, i], KC)
    shared["lnT"] = ln.reshape(128, -1)
    shared["sink"] = f32(inp["attn_sink"]).reshape(1, -1)
    shared["sgu_nwb"] = np.ascontiguousarray(np.concatenate([f32(inp["sgu_norm_w"]), f32(inp["sgu_norm_b"])], axis=1))
    ws = f32(inp["sgu_ws"])
    shared["wsT"] = np.ascontiguousarray(ws.transpose(3, 0, 1, 2)).reshape(128, -1)
    bs = f32(inp["sgu_bs"])
    shared["bsT"] = np.ascontiguousarray(np.broadcast_to(bs[None], (64,) + bs.shape)).reshape(64, -1)
    dec = f32(inp["ret_decay"])
    shared["dec_row"] = dec.reshape(1, -1)
    dcol = np.zeros((128, DEPTH, 2, 2), np.float32)
    for hh in range(2):
        for pr in range(2):
            dcol[hh * 64:(hh + 1) * 64, :, :, pr] = dec[None, :, :, pr * 2 + hh]
    shared["dec_col"] = dcol.reshape(128, -1)
    gn = f32(inp["ret_gn_w"])
    gcol = np.zeros((128, DEPTH, 2, 2), np.float32)
    for l in range(DEPTH):
        for dr in range(2):
            gcol[:, l, dr, :] = _fm(gn[l, dr], 2)
    shared["gnw"] = gcol.reshape(128, -1)
    shared["poolw"] = f32(inp["pool_w"])
    psc = f32(inp["pool_scale"])
    shared["pool_scaleT"] = np.ascontiguousarray(np.concatenate([_fm(psc[l], 2) for l in range(DEPTH)], axis=1))

    in_maps = []
    for core in range(NCORE):
        b = core % 2
        m = dict(shared)
        xp = x_prompt[2 * core:2 * core + 2].reshape(512, D)
        m["xpT"] = np.ascontiguousarray(xp.T)
        m["xsT"] = np.ascontiguousarray(x_sample[b].T)
        cond = np.stack([c_ctx, c[b]], axis=1)
        m["cond"] = np.ascontiguousarray(cond.reshape(KC, 128, 2).transpose(1, 0, 2)).reshape(128, -1)
        kk = ck[b].reshape(DEPTH, 256, 128)
        m["kctxT"] = np.ascontiguousarray(kk.transpose(2, 0, 1)).reshape(128, -1)
        vv = cv[b].reshape(DEPTH, 2, 128, 128)
        m["vctx"] = np.ascontiguousarray(vv.transpose(2, 0, 1, 3)).reshape(128, -1)
        m["state0"] = np.ascontiguousarray(st[b])
        in_maps.append(m)
    return in_maps


def kernel(**inputs):
    if "nc" not in _CACHE:
        _CACHE["nc"] = build_program()
    nc = _CACHE["nc"]
    in_maps = _prep_inputs(inputs)
    res = run_bass_kernel_spmd(nc, in_maps, core_ids=list(range(NCORE)))
    R = res.results
    B, SEQ = 16, 256
    y_prompt = np.zeros((B, SEQ, D), np.float32)
    new_k = np.zeros((B, DEPTH, SEQ, 2, 64), np.float32)
    new_v = np.zeros((B, DEPTH, SEQ, 2, 64), np.float32)
    new_s = np.zeros((B, DEPTH, 2, 4, 64, 64), np.float32)
    y_sample = np.zeros((2, 2048, D), np.float32)
    for core in range(NCORE):
        r = R[core]
        yp = np.asarray(r["ypT"]).T.reshape(2, SEQ, D)
        y_prompt[2 * core:2 * core + 2] = yp
        ckk = np.asarray(r["ck_out"]).reshape(DEPTH, 2, SEQ, 2, 64)
        cvv = np.asarray(r["cv_out"]).reshape(DEPTH, 2, SEQ, 2, 64)
        new_k[2 * core:2 * core + 2] = ckk.transpose(1, 0, 2, 3, 4)
        new_v[2 * core:2 * core + 2] = cvv.transpose(1, 0, 2, 3, 4)
        so = np.asarray(r["st_out"])
        new_s[2 * core:2 * core + 2] = so.transpose(1, 0, 2, 3, 4, 5)
        if core < 2:
            y_sample[core] = np.asarray(r["ysT"]).T
    return (y_prompt, y_sample, new_k, new_v, new_s)
```

```python
import os
import numpy as np
import concourse.bass as bass
import concourse.mybir as mybir
from concourse.bass_utils import run_bass_kernel_spmd

F32 = mybir.dt.float32
BF16 = mybir.dt.bfloat16
AF = mybir.ActivationFunctionType
ALU = mybir.AluOpType

D = 1024
KC = 8
DEPTH = 2
NCORE = 8
DFF = 2816
NFC = 22
ALPHA = float((2.0 * DEPTH) ** 0.25)
EPS = 1e-5
WIN = 15 * 256
NSLOT = 3
SLOTW = 2816
NDSEM = {"sp": 10, "pool": 4}


class Sched:
    ENG = ("pe", "act", "dve", "pool", "sp")

    def __init__(self):
        self.q = {e: [] for e in self.ENG}
        self.res = {}
        self.dma_rr = {e: 0 for e in self.ENG}
        self.dma_cnt = {}
        self.last_ev = {}
        self.pending = {}

    def add(self, eng, fn, r=(), w=(), dma=False):
        idx = len(self.q[eng])
        deps = set()
        for k in r:
            e = self.res.get(k)
            if e is not None and e[0] is not None:
                deps.add(e[0])
            if e is not None and k.startswith("ps"):
                for rd in e[1]:
                    if not (rd[0] == "E" and rd[1] == eng):
                        deps.add(rd)
        raw = set(deps)
        for k in w:
            e = self.res.get(k)
            if e is not None:
                if e[0] is not None:
                    deps.add(e[0])
                deps.update(e[1])
        if eng in self.pending:
            deps.update(self.pending.pop(eng))
        if dma:
            sk = (eng, self.dma_rr[eng] % NDSEM[eng])
            self.dma_rr[eng] += 1
            prev = self.dma_cnt.get(sk, 0)
            if prev > 0:
                deps.add(("D", sk, prev))
            cnt = prev + 16
            self.dma_cnt[sk] = cnt
            ev = ("D", sk, cnt)
            self.last_ev[sk] = ev
        else:
            ev = ("E", eng, idx)
            self.last_ev[eng] = ev
        d2 = []
        for d in deps:
            if d[0] == "E" and d[1] == eng and eng == "pe":
                continue
            d2.append(d)
        self.q[eng].append((fn, d2, ev, dma))
        for k in r:
            self.res.setdefault(k, [None, []])[1].append(ev)
        for k in w:
            self.res[k] = [ev, []]
        return ev

    def barrier(self):
        evs = set(self.last_ev.values())
        for e in self.ENG:
            if e == "pool":
                continue
            self.pending.setdefault(e, set()).update(evs)

    def emit(self, nc, block, semobj):
        needed = set()
        for e in self.ENG:
            for (_, deps, _, _) in self.q[e]:
                for d in deps:
                    if d[0] == "E":
                        needed.add(d)
        lastc = {}
        for e in self.ENG:
            for (_, _, ev, dma) in self.q[e]:
                if not dma:
                    lastc[e] = ev
        needed.update(lastc.values())
        val = {}
        for e in self.ENG:
            c = 0
            for i, (_, _, ev, dma) in enumerate(self.q[e]):
                if (not dma) and ev in needed:
                    c += 1
                    val[ev] = c
        final = dict(self.dma_cnt)

        def run(eng, h):
            waited = {}
            for (fn, deps, ev, dma) in self.q[eng]:
                ws = {}
                for d in deps:
                    if d[0] == "E":
                        sk, v = d[1], val[d]
                    else:
                        sk, v = d[1], d[2]
                    if v > ws.get(sk, 0):
                        ws[sk] = v
                for sk, v in ws.items():
                    if waited.get(sk, 0) >= v:
                        continue
                    h.wait_ge(semobj[sk], v)
                    waited[sk] = v
                ins = fn(h)
                if dma:
                    ins.then_inc(semobj[ev[1]], 16)
                elif ev in needed:
                    ins.then_inc(semobj[eng], 1)
            if eng == "sp":
                for sk, v in final.items():
                    if waited.get(sk, 0) < v:
                        h.wait_ge(semobj[sk], v)
                for e2, ev2 in lastc.items():
                    if e2 != "sp":
                        h.wait_ge(semobj[e2], val[ev2])

        @block.tensor
        def _(h):
            run("pe", h)

        @block.scalar
        def _(h):
            run("act", h)

        @block.vector
        def _(h):
            run("dve", h)

        @block.gpsimd
        def _(h):
            run("pool", h)

        @block.sync
        def _(h):
            run("sp", h)


def _perm64():
    d = np.arange(64)
    return np.where((d % 32) < 16, d + 16, d - 16)


def _win_ext_cols():
    qa, ka, va, ub, vb, qc, kc, vc, gcf, gcb, pd = (0, 256, 384, 512, 768, 1024, 1280, 1536, 1792, 2048, 2304)
    p = _perm64()
    r = np.arange
    cols = []
    cols += list(ka + r(128))
    cols += list(ka + np.concatenate([p, 64 + p]))
    cols += list(pd + r(256))
    cols += list(va + r(128)) + list(ka + r(128))
    cols += list(kc + r(256))
    cols += list(vc + r(256))
    qg = [np.concatenate([qa + (0 * 2 + g) * 64 + r(64), qa + (1 * 2 + g) * 64 + r(64)]) for g in (0, 1)]
    qgp = [np.concatenate([qa + (0 * 2 + g) * 64 + p, qa + (1 * 2 + g) * 64 + p]) for g in (0, 1)]
    cols += list(qg[0]) + list(qg[1])
    cols += list(qgp[0]) + list(qgp[1])
    cols += list(ub + r(256))
    cols += list(qc + r(256))
    cols += list(kc + r(256))
    cols += list(gcf + r(256))
    cols += list(gcb + r(256))
    cols += list(vb + r(256))
    cols += list(kc + r(256))
    cols += list(vc + r(256))
    cols = np.asarray(cols)
    assert cols.shape[0] == WIN
    return cols


def _up_ext_cols():
    cols = []
    for i in range(NFC):
        cols += list(i * 128 + np.arange(128)) + list(DFF + i * 128 + np.arange(128))
    return np.asarray(cols)


def _fm(v, nch):
    return np.ascontiguousarray(v.reshape(nch, 128).T)


def _const_tables():
    j = np.arange(128, dtype=np.float32)[:, None]
    i = np.arange(128, dtype=np.float32)[None, :]
    c = {}
    c["relF"] = np.maximum(i - j, 0.0)
    c["relB"] = np.maximum(j - i, 0.0)
    c["maskF8"] = (i >= j).astype(np.float32) * 0.125
    c["maskB8"] = (j >= i).astype(np.float32) * 0.125
    c["iota1"] = np.broadcast_to(i + 1.0, (128, 128)).copy()
    c["iotaB"] = np.broadcast_to(128.0 - i, (128, 128)).copy()
    big = np.concatenate([c["relF"], c["relB"], c["maskF8"], c["maskB8"], c["iota1"], c["iotaB"]], axis=1)
    small = np.zeros((128, 64), np.float32)
    small[:, 0] = 127.0 - np.arange(128)
    small[:, 1] = np.arange(128)
    wins = np.array([[2, 4], [8, 16]], np.float32)
    for c2 in range(2):
        for half in range(2):
            w = wins[c2, half]
            rows = slice(half * 64, (half + 1) * 64)
            small[rows, 2 + c2] = 1.0 / w
            for e in range(8):
                cnt_s = min(w, e + w / 2)
                small[rows, 4 + c2 * 8 + e] = w / cnt_s
                dist = 8 - e
                cnt_e = (dist + w / 2) if dist < w / 2 else w
                small[rows, 20 + c2 * 8 + e] = w / cnt_e
    bd = np.zeros((128, 128), np.float32)
    bd[:64, :64] = 1.0
    bd[64:, 64:] = 1.0
    jj = np.arange(128)[:, None]
    ii = np.arange(128)[None, :]
    mprev = (jj >= ii).astype(np.float32)
    mnext = (jj <= ii).astype(np.float32)
    cb = np.concatenate([np.full((128, 128), 1.0 / 1024, np.float32), bd / 64.0, np.ones((128, 64), np.float32),
                         mprev, mprev, mnext, mnext], axis=1)
    bd2 = np.concatenate([bd, bd], axis=1)
    return big, small, bd2, cb


def _rope_tables():
    n = 2048
    t = np.arange(n)
    row = (t // 64).astype(np.float32)
    col = (t % 64).astype(np.float32)
    half = 32
    freqs = (10000.0 ** (-np.arange(0, half, 2, dtype=np.float32) / half)).astype(np.float32)
    C = np.zeros((64, n), np.float32)
    S = np.zeros((64, n), np.float32)
    for d in range(64):
        pos = row if d < 32 else col
        f = freqs[d % 16]
        ang = (pos * f).astype(np.float32)
        C[d] = np.cos(ang)
        S[d] = np.sin(ang) * (-1.0 if (d % 32) < 16 else 1.0)
    return np.concatenate([C, C], 0), np.concatenate([S, S], 0)


class Group:
    def __init__(self, name, ntile, seqlen, ci, sample):
        self.name = name
        self.ntile = ntile
        self.nt = ntile * 512
        self.seqlen = seqlen
        self.nseq = self.nt // seqlen
        self.ci = ci
        self.sample = sample
        self.nchunk = ntile * 4
        self.cps = seqlen // 128

    def segs(self, t):
        out = []
        a = t * 512
        while a < (t + 1) * 512:
            s = a // self.seqlen
            e = min((s + 1) * self.seqlen, (t + 1) * 512)
            out.append((a - t * 512, e - a, s, a - s * self.seqlen))
            a = e
        return out


class _Stop(Exception):
    pass


def build_program(debug=None, stop=None):
    nc = bass.Bass("TRN2", target_bir_lowering=False)
    S = Sched()

    def din(name, shape):
        return nc.dram_tensor(name, list(shape), F32, kind="ExternalInput").ap()

    def dout(name, shape):
        return nc.dram_tensor(name, list(shape), F32, kind="ExternalOutput").ap()

    d_xp = din("xpT", [D, 512])
    d_xs = din("xsT", [D, 2048])
    d_cond = din("cond", [128, KC * 2])
    d_wada = din("w_ada", [DEPTH, D, 6 * D])
    d_bada = din("b_adaT", [128, DEPTH * 48])
    d_win = din("w_in_ext", [DEPTH, D, WIN])
    d_wout = din("w_out", [DEPTH, D, D])
    d_up = din("ffn_up_ext", [DEPTH, D, 2 * DFF])
    d_down = din("ffn_down", [DEPTH, DFF, D])
    d_conv = din("convT", [128, DEPTH * 4 * 44])
    d_ln = din("lnT", [128, DEPTH * 2 * 2 * KC])
    d_sink = din("sink", [1, DEPTH * 4])
    d_nwb = din("sgu_nwb", [DEPTH, 512])
    d_wsT = din("wsT", [128, DEPTH * 4 * 128])
    d_bsT = din("bsT", [64, DEPTH * 4 * 128])
    d_dec_row = din("dec_row", [1, DEPTH * 8])
    d_dec_col = din("dec_col", [128, DEPTH * 4])
    d_gnw = din("gnw", [128, DEPTH * 4])
    d_poolw = din("poolw", [DEPTH, 4, 64, 64])
    d_pscale = din("pool_scaleT", [128, DEPTH * 2])
    d_kctx = din("kctxT", [128, DEPTH * 256])
    d_vctx = din("vctx", [128, DEPTH * 2 * 128])
    d_st0 = din("state0", [DEPTH, 2, 4, 64, 64])
    d_cbig = din("c_big", [128, 768])
    d_csmall = din("c_small", [128, 64])
    d_bd2 = din("c_bd2", [128, 256])
    d_cb = din("c_bf", [128, 832])
    d_ropeC = din("ropeC", [128, 2048])
    d_ropeS = din("ropeS", [128, 2048])

    o_yp = dout("ypT", [D, 512])
    o_ys = dout("ysT", [D, 2048])
    o_ck = dout("ck_out", [DEPTH, 512, 128])
    o_cv = dout("cv_out", [DEPTH, 512, 128])
    o_st = dout("st_out", [DEPTH, 2, 2, 4, 64, 64])

    dbg = {}
    if debug:
        for name, shape in debug.items():
            dbg[name] = dout("dbg_" + name, shape)

    NF = 26400
    NB = 53600
    from contextlib import ExitStack
    es = ExitStack()
    fpool = es.enter_context(nc.sbuf_tensor("fpool", [128, NF], F32))
    bpool = es.enter_context(nc.sbuf_tensor("bpool", [128, NB], BF16))
    psum = es.enter_context(nc.psum_tensor("psum", [128, 4096], F32))
    semobj = {}
    for e in Sched.ENG:
        semobj[e] = es.enter_context(nc.semaphore("s_" + e))
        for i in range(NDSEM.get(e, 0)):
            semobj[(e, i)] = es.enter_context(nc.semaphore("d_%s%d" % (e, i)))
    block = es.enter_context(nc.Block())

    class Alloc:
        def __init__(self, t, size):
            self.t, self.size, self.off = t, size, 0

        def __call__(self, n, **dims):
            a = self.t[:, self.off:self.off + n]
            self.off += n
            assert self.off <= self.size, (self.off, self.size)
            return a

    fa = Alloc(fpool, NF)
    ba = Alloc(bpool, NB)

    def v3(ap, a):
        return ap.rearrange("p (a b) -> p a b", a=a)

    def v4(ap, a, b):
        return ap.rearrange("p (a b c) -> p a b c", a=a, b=b)

    X = v3(fa(KC * 2048), KC)
    csm = fa(64)
    bd2 = fa(256)
    modv = fa(DEPTH * 2 * 48)
    dvec = fa(DEPTH * 2 * 2 * KC)
    lvec = fa(DEPTH * 4 * KC)
    badaT = fa(DEPTH * 48)
    convT = fa(DEPTH * 4 * 44)
    lnT = fa(DEPTH * 2 * 2 * KC)
    sinkb = fa(DEPTH * 4)
    esink = fa(DEPTH * 4)
    decrow = fa(DEPTH * 8)
    lgrow = fa(DEPTH * 8)
    deccol = fa(DEPTH * 4)
    lgcol = fa(DEPTH * 4)
    gnw = fa(DEPTH * 4)
    pscale = fa(DEPTH * 2)
    condf = fa(KC * 2)
    nwb = fa(512)
    bsT = fa(512)
    DfT = fa(512)
    DbT = fa(512)
    qdecF = fa(256)
    qdecB = fa(256)
    kdec = fa(8)
    cdec = fa(4)
    SrunF = fa(256)
    SrunB = fa(256)
    S0 = fa(512)
    ropeC = fa(512)
    ropeS = fa(512)
    stg = fa(256)
    stgS = fa(128)
    lnst = fa(16)
    lnm = fa(512)
    lnv = fa(512)
    NFS = 5
    fscr = [fa(512) for _ in range(NFS)]
    fpers = fa.off

    wbuf = [ba(SLOTW) for _ in range(NSLOT)]
    cbf = ba(832)
    onesmean = cbf[:, 0:128]
    bd64 = cbf[:, 128:256]
    ones64 = cbf[:, 256:320]
    mprev = cbf[:, 320:576]
    mnext = cbf[:, 576:832]
    siluc = ba(KC * 2)
    wsT = ba(DEPTH * 4 * 128)
    poolW = ba(DEPTH * 2 * 128)
    kctx = ba(DEPTH * 256)
    vctx = ba(DEPTH * 2 * 128)
    NBS = 5
    bscr = [ba(512) for _ in range(NBS)]
    NER = 10
    ering = [ba(256) for _ in range(NER)]
    er_i = [0]
    sfr = [ba(256) for _ in range(4)]
    bpers = ba.off

    hA = v3(ba(KC * 512), KC)
    yT = hA
    kaT_g = ba(2048)
    va_g = v3(ba(16 * 128), 16)
    pd_g = v3(ba(2 * 2080), 2)
    Sb_g = v3(ba(16 * 256), 16)
    t2_off = ba.off
    qaT = v3(ba(2 * 512), 2)
    ubT = v3(ba(4 * 512), 4)
    qcT = v3(ba(2 * 512), 2)
    kcT = v3(ba(2 * 512), 2)
    qdf = v3(ba(2 * 512), 2)
    qdb = v3(ba(2 * 512), 2)
    sgf = v3(ba(2 * 512), 2)
    sgb = v3(ba(2 * 512), 2)
    vbn = v3(ba(4 * 256), 4)
    vz = ba(4 * 512)
    vcb = v3(ba(4 * 256), 4)
    kdf = v3(ba(4 * 256), 4)
    scf = [ba(512) for _ in range(2)]
    scb = [ba(512) for _ in range(2)]
    kdb1_flat = ba(4 * 256)
    vcb1_flat = ba(4 * 256)
    kdb1 = v3(kdb1_flat, 4)
    vcb1 = v3(vcb1_flat, 4)
    yA = v3(kdb1_flat, 2)
    yD = v3(vcb1_flat, 2)
    b1_end = ba.off
    ba.off = bpers
    hF = v3(ba(KC * 1024), KC)
    hFh = v3(ba(KC * 2), KC)
    hid = v3(ba(NFC * 1024), NFC)
    b2_end = ba.off
    assert max(b1_end, b2_end) <= NB, (b1_end, b2_end, NB)
    assert fpers <= NF, (fpers, NF)

    def PS(b, n0=0, n1=512):
        return psum[:, b * 512 + n0:b * 512 + n1]

    def pk(b):
        return "ps%d" % b

    class RR:
        def __init__(self, banks):
            self.banks, self.i = list(banks), 0

        def __call__(self):
            b = self.banks[self.i % len(self.banks)]
            self.i += 1
            return b

    rr_all = RR(range(8))
    rr_lo = RR(range(4))

    fs_i = [0]
    bs_i = [0]

    def fsc():
        i = fs_i[0] % NFS
        fs_i[0] += 1
        return fscr[i], "fs%d" % i

    def bsc():
        i = bs_i[0] % NBS
        bs_i[0] += 1
        return bscr[i], "bs%d" % i

    def MM(out, lhsT, rhs, start, stop, r, w):
        S.add("pe", lambda h: h.matmul(out, lhsT=lhsT, rhs=rhs, start=start, stop=stop), r, w)

    def ACT(out, in_, func, r, w, scale=None, bias=None):
        kw = {}
        if scale is not None:
            kw["scale"] = scale
        if bias is not None:
            kw["bias"] = bias
        S.add("act", lambda h: h.activation(out=out, in_=in_, func=func, **kw), r, w)

    def TT(out, a, b, op, r, w):
        S.add("dve", lambda h: h.tensor_tensor(out=out, in0=a, in1=b, op=op), r, w)

    def TS(out, a, s1, s2, op0, op1, r, w):
        if s2 is None:
            S.add("dve", lambda h: h.tensor_scalar(out=out, in0=a, scalar1=s1, scalar2=None, op0=op0), r, w)
        else:
            S.add("dve", lambda h: h.tensor_scalar(out=out, in0=a, scalar1=s1, scalar2=s2, op0=op0, op1=op1), r, w)

    def STT(out, a, s, b, op0, op1, r, w):
        S.add("dve", lambda h: h.scalar_tensor_tensor(out=out, in0=a, scalar=s, in1=b, op0=op0, op1=op1), r, w)

    def VCP(out, in_, r, w):
        S.add("dve", lambda h: h.tensor_scalar(out=out, in0=in_, scalar1=1.0, scalar2=None, op0=ALU.mult), r, w)

    def VMEMSET(ap, val, w):
        S.add("dve", lambda h: h.memset(ap, val), (), w)

    def DMA(out, in_, r, w, eng="sp"):
        S.add(eng, lambda h: h.dma_start(out=out, in_=in_), r, w, dma=True)

    wl_i = [0]

    def wload(src, kch, cols):
        i = wl_i[0] % NSLOT
        wl_i[0] += 1
        dst = wbuf[i][:, 0:kch * cols].rearrange("p (k c) -> p k c", k=kch)
        s = src.rearrange("(k p) c -> p k c", p=128)
        DMA(dst, s, (), ["w%d" % i], eng="pool")
        return dst, "w%d" % i

    def xk(kc, t):
        return "x%d_%d" % (kc, t)

    def hk(kc):
        return "h%d" % kc

    def ckpt(name):
        if stop == name:
            raise _Stop()

    try:
        DMA(csm, d_csmall, (), ["csm"])
        DMA(bd2, d_bd2, (), ["bd2"])
        DMA(cbf, d_cb, (), ["cbf"], eng="pool")
        DMA(badaT, d_bada, (), ["badaT"])
        DMA(convT, d_conv, (), ["convT"])
        DMA(lnT, d_ln, (), ["lnT"])
        DMA(sinkb, d_sink.partition_broadcast(128), (), ["sinkb"])
        DMA(decrow, d_dec_row.partition_broadcast(128), (), ["decrow"])
        DMA(deccol, d_dec_col, (), ["deccol"])
        DMA(gnw, d_gnw, (), ["gnw"])
        DMA(pscale, d_pscale, (), ["pscale"])
        DMA(condf, d_cond, (), ["condf"])
        DMA(wsT, d_wsT, (), ["wsT"], eng="pool")
        DMA(kctx, d_kctx, (), ["kctx"], eng="pool")
        DMA(vctx, d_vctx, (), ["vctx"], eng="pool")
        VMEMSET(poolW, 0.0, ["poolW"])
        pw4 = v4(poolW, DEPTH, 2)
        for l in range(DEPTH):
            for g4 in range(4):
                c2, gg = g4 // 2, g4 % 2
                DMA(pw4[gg * 64:(gg + 1) * 64, l, c2, gg * 64:(gg + 1) * 64], d_poolw[l, g4], (), ["poolW"], eng="pool")

        col127 = csm[:, 0:1]
        colj = csm[:, 1:2]
        invw = csm[:, 2:4]
        corrS = v3(csm[:, 4:20], 2)
        corrE = v3(csm[:, 20:36], 2)

        f, fk = fsc()
        ACT(f[:, 0:16], decrow, AF.Exp, ["decrow"], [fk], scale=-1.0)
        ACT(f[:, 16:32], f[:, 0:16], AF.Ln, [fk], [fk], bias=1.0)
        TS(lgrow, f[:, 16:32], -1.0, None, ALU.mult, ALU.bypass, [fk], ["lgrow"])
        ACT(f[:, 32:40], deccol, AF.Exp, ["deccol"], [fk], scale=-1.0)
        ACT(f[:, 40:48], f[:, 32:40], AF.Ln, [fk], [fk], bias=1.0)
        TS(lgcol, f[:, 40:48], -1.0, None, ALU.mult, ALU.bypass, [fk], ["lgcol"])
        ACT(esink, sinkb, AF.Exp, ["sinkb"], ["esink"])
        ACT(siluc, condf, AF.Silu, ["condf"], ["siluc"])
        siluc3 = v3(siluc, KC)
        ckpt("consts")

        modv4 = v4(modv, DEPTH, 2)
        dvec4 = v4(dvec, DEPTH, 2)
        lvec3 = v3(lvec, DEPTH)
        lnT5 = lnT.rearrange("p (l i j k) -> p l i j k", l=DEPTH, i=2, j=2)

        def do_mod(l):
            b = rr_all()
            for blk in range(24):
                w, wk = wload(d_wada[l][:, blk * 256:(blk + 1) * 256], KC, 256)
                for j in range(2):
                    n = blk * 2 + j
                    for kc in range(KC):
                        MM(PS(b, n * 2, n * 2 + 2), w[:, kc, j * 128:(j + 1) * 128], siluc3[:, kc, :], kc == 0, kc == KC - 1,
                           [wk, "siluc"], [pk(b)])
            pv = PS(b, 0, 96).rearrange("p (n c) -> p n c", c=2)
            for ci in range(2):
                TT(modv4[:, l, ci, :], pv[:, :, ci], badaT[:, l * 48:(l + 1) * 48], ALU.add, [pk(b), "badaT"], ["modv%d" % l])
            for ci in range(2):
                TS(dvec4[:, l, ci, 0:KC], modv4[:, l, ci, 8:16], 1.0, 1.0 / ALPHA, ALU.add, ALU.mult, ["modv%d" % l], ["dvec%d" % l])
                TS(dvec4[:, l, ci, KC:2 * KC], modv4[:, l, ci, 32:40], 1.0, 1.0 / ALPHA, ALU.add, ALU.mult, ["modv%d" % l], ["dvec%d" % l])
            TS(lvec3[:, l, 0:16], lnT5[:, l, 0].rearrange("p j k -> p (j k)"), ALPHA, None, ALU.mult, ALU.bypass, ["lnT"], ["lvec%d" % l])
            a2 = ALPHA if l < DEPTH - 1 else 1.0
            TS(lvec3[:, l, 16:32], lnT5[:, l, 1].rearrange("p j k -> p (j k)"), a2, None, ALU.mult, ALU.bypass, ["lnT"], ["lvec%d" % l])

        do_mod(0)
        ckpt("mod")

        def mvec(l, ci, q):
            return modv4[:, l, ci, q * 8:(q + 1) * 8]

        def make_h(dst, l, g, t, which, ncols=512, col0=None, dkeys=None):
            c0 = t * 512 if col0 is None else col0
            sv = dvec4[:, l, g.ci, 0:KC] if which == "a" else dvec4[:, l, g.ci, KC:2 * KC]
            bv = mvec(l, g.ci, 0 if which == "a" else 3)
            for kc in range(KC):
                tt = (c0 // 512)
                ACT(dst[:, kc, :], X[:, kc, c0:c0 + ncols], AF.Identity, [xk(kc, tt), "dvec%d" % l, "modv%d" % l],
                    [dkeys[kc] if dkeys else hk(kc)], scale=sv[:, kc:kc + 1], bias=bv[:, kc:kc + 1])

        def ln_stats(t, kc, bM, bQ):
            tc = slice(t * 512, (t + 1) * 512)
            tb, tbk = bsc()
            tq, tqk = bsc()
            ACT(tb, X[:, kc, tc], AF.Copy, [xk(kc, t)], [tbk])
            ACT(tq, X[:, kc, tc], AF.Square, [xk(kc, t)], [tqk])
            MM(PS(bM), onesmean, tb, kc == 0, kc == KC - 1, ["cbf", tbk], [pk(bM)])
            MM(PS(bQ), onesmean, tq, kc == 0, kc == KC - 1, ["cbf", tqk], [pk(bQ)])

        def ln_finish(t, wv, bv, bM, bQ):
            tc = slice(t * 512, (t + 1) * 512)
            fm, fmk = lnm, "lnm"
            fv, fvk = lnv, "lnv"
            ACT(fm, PS(bM), AF.Copy, [pk(bM)], [fmk])
            ACT(fv, PS(bM), AF.Square, [pk(bM)], [fvk])
            TT(fv, PS(bQ), fv, ALU.subtract, [pk(bQ), fvk], [fvk])
            ACT(fv, fv, AF.Ln, [fvk], [fvk], bias=EPS)
            ACT(fv, fv, AF.Exp, [fvk], [fvk], scale=-0.5)
            for k0 in range(0, KC, 2):
                us = [fsc(), fsc()]
                for j_, (u, uk) in enumerate(us):
                    TT(u, X[:, k0 + j_, tc], fm, ALU.subtract, [xk(k0 + j_, t), fmk], [uk])
                for j_, (u, uk) in enumerate(us):
                    TT(u, u, fv, ALU.mult, [uk, fvk], [uk])
                for j_, (u, uk) in enumerate(us):
                    kc = k0 + j_
                    ACT(X[:, kc, tc], u, AF.Identity, [uk, "lvec0", "lvec1"], [xk(kc, t)], scale=wv[:, kc:kc + 1], bias=bv[:, kc:kc + 1])

        def dbg_out(name, ap, keys):
            if name in dbg:
                DMA(dbg[name], ap, keys, [])

        stgS3 = v3(stgS, 2)

        def state_out(S3, key, l, sq, dr):
            for pr in range(2):
                for hh in range(2):
                    TS(stgS3[hh * 64:(hh + 1) * 64, pr, :], S3[hh * 64:(hh + 1) * 64, pr, hh * 64:(hh + 1) * 64], 1.0, None,
                       ALU.mult, ALU.bypass, [key], ["stgS"])
            DMA(o_st[l, sq, dr].rearrange("(pr hh) d e -> (hh d) pr e", pr=2), stgS3, ["stgS"], [])

        groups = [Group("p", 1, 256, 0, False), Group("s", 4, 2048, 1, True)]
        vz5 = vz.rearrange("p (c a h e) -> p c a h e", c=4, a=2, h=2)
        wsT4 = v4(wsT, DEPTH, 4)
        bsT3 = v3(bsT, 4)
        kctx3 = v3(kctx, DEPTH)
        vctx4 = v4(vctx, DEPTH, 2)
        esink3 = v3(esink, DEPTH)
        gnw4 = v4(gnw, DEPTH, 2)
        pscale3 = v3(pscale, DEPTH)
        convT4 = v4(convT, DEPTH, 4)
        lgrow4 = v4(lgrow, DEPTH, 2)
        lgcol4 = v4(lgcol, DEPTH, 2)
        kdec3 = v3(kdec, 2)
        cdec3 = v3(cdec, 2)
        DfT3 = v3(DfT, 4)
        DbT3 = v3(DbT, 4)
        qdecF3 = v3(qdecF, 2)
        qdecB3 = v3(qdecB, 2)
        SrunF3 = v3(SrunF, 2)
        SrunB3 = v3(SrunB, 2)
        S03 = v3(S0, 2)
        pw4 = v4(poolW, DEPTH, 2)

        for g in groups:
            S.barrier()
            d_x = d_xs if g.sample else d_xp
            o_y = o_ys if g.sample else o_yp
            for t in range(g.ntile):
                for kc in range(KC):
                    DMA(X[:, kc, t * 512:(t + 1) * 512], d_x[kc * 128:(kc + 1) * 128, t * 512:(t + 1) * 512], (), [xk(kc, t)])
                    ACT(X[:, kc, t * 512:(t + 1) * 512], X[:, kc, t * 512:(t + 1) * 512], AF.Identity, [xk(kc, t)], [xk(kc, t)], scale=ALPHA)

            ckpt("xload")
            for l in range(DEPTH):
                last = (l == DEPTH - 1)
                cb_, cbk = fsc()
                cb2_, cb2k = fsc()
                DMA(cb_, d_cbig[:, 0:512], (), [cbk])
                DMA(cb2_[:, 0:256], d_cbig[:, 512:768], (), [cb2k])
                DMA(nwb, d_nwb[l:l + 1, :].partition_broadcast(128), (), ["nwb"])
                DMA(bsT[0:64, :], d_bsT[:, l * 512:(l + 1) * 512], (), ["bsT"])
                for h4 in range(4):
                    ACT(DfT3[:, h4, :], cb_[:, 0:128], AF.Exp, [cbk, "lgrow"], ["DfT"], scale=lgrow4[:, l, 0, h4:h4 + 1])
                    ACT(DbT3[:, h4, :], cb_[:, 128:256], AF.Exp, [cbk, "lgrow"], ["DbT"], scale=lgrow4[:, l, 1, h4:h4 + 1])
                TT(DfT3, DfT3, cb_[:, 256:384].unsqueeze(1).to_broadcast([128, 4, 128]), ALU.mult, ["DfT", cbk], ["DfT"])
                TT(DbT3, DbT3, cb_[:, 384:512].unsqueeze(1).to_broadcast([128, 4, 128]), ALU.mult, ["DbT", cbk], ["DbT"])
                for pr in range(2):
                    ACT(qdecF3[:, pr, :], cb2_[:, 0:128], AF.Exp, [cb2k, "lgcol"], ["qdecF"], scale=lgcol4[:, l, 0, pr:pr + 1])
                    ACT(qdecB3[:, pr, :], cb2_[:, 128:256], AF.Exp, [cb2k, "lgcol"], ["qdecB"], scale=lgcol4[:, l, 1, pr:pr + 1])
                ACT(kdec3[:, 0, :], lgrow4[:, l, 0, :], AF.Exp, ["lgrow", "csm"], ["kdec"], scale=col127)
                ACT(kdec3[:, 1, :], lgrow4[:, l, 1, :], AF.Exp, ["lgrow", "csm"], ["kdec"], scale=colj)
                TS(kdec, kdec, 0.125, None, ALU.mult, ALU.bypass, ["kdec"], ["kdec"])
                ACT(cdec3[:, 0, :], lgcol4[:, l, 0, :], AF.Exp, ["lgcol"], ["cdec"], scale=128.0)
                ACT(cdec3[:, 1, :], lgcol4[:, l, 1, :], AF.Exp, ["lgcol"], ["cdec"], scale=128.0)
                if g.sample:
                    VMEMSET(S0, 0.0, ["S0"])
                    S04 = v4(S0, 2, 2)
                    for dr in range(2):
                        DMA(stgS3, d_st0[l, dr].rearrange("(pr hh) d e -> (hh d) pr e", pr=2), (), ["stgS"])
                        for pr in range(2):
                            for hh in range(2):
                                TS(S04[hh * 64:(hh + 1) * 64, dr, pr, hh * 64:(hh + 1) * 64], stgS3[hh * 64:(hh + 1) * 64, pr, :], 1.0, None,
                                   ALU.mult, ALU.bypass, ["stgS"], ["S0"])

                ckpt("tables")
                VMEMSET(pd_g.rearrange("p a b -> p (a b)"), 0.0, ["pd"])
                order1 = list(reversed(range(g.ntile)))
                make_h(hA, l, g, order1[0], "a")
                if g.sample:
                    DMA(ropeC, d_ropeC[:, order1[0] * 512:(order1[0] + 1) * 512], (), ["ropeC"])
                    DMA(ropeS, d_ropeS[:, order1[0] * 512:(order1[0] + 1) * 512], (), ["ropeS"])
                pend = None
                for i1, t in enumerate(order1):
                    tcs = slice(t * 512, (t + 1) * 512)
                    hr = [hk(kc) for kc in range(KC)]
                    w, wk = wload(d_win[l][:, 0:256], KC, 256)
                    b1 = rr_all()
                    for kc in range(KC):
                        MM(PS(b1), w[:, kc, 0:128], hA[:, kc, :], kc == 0, kc == KC - 1, [wk, hk(kc)], [pk(b1)])
                    if g.sample:
                        b2 = rr_all()
                        for kc in range(KC):
                            MM(PS(b2), w[:, kc, 128:256], hA[:, kc, :], kc == 0, kc == KC - 1, [wk, hk(kc)], [pk(b2)])
                        f1, f1k = fsc()
                        f2, f2k = fsc()
                        TT(f1, PS(b1), ropeC, ALU.mult, [pk(b1), "ropeC"], [f1k])
                        TT(f2, PS(b2), ropeS, ALU.mult, [pk(b2), "ropeS"], [f2k])
                        TT(kaT_g[:, tcs], f1, f2, ALU.add, [f1k, f2k], ["kaT%d" % t])
                    else:
                        ACT(kaT_g[:, tcs], PS(b1), AF.Copy, [pk(b1)], ["kaT%d" % t])
                    if pend is not None:
                        next(pend, None)
                    ckpt("b0")
                    w, wk = wload(d_win[l][:, 256:512], KC, 256)
                    for c2 in range(2):
                        b = rr_all()
                        for kc in range(KC):
                            MM(PS(b), w[:, kc, c2 * 128:(c2 + 1) * 128], hA[:, kc, :], kc == 0, kc == KC - 1, [wk, hk(kc)], [pk(b)])
                        for (c0, n, sq, tau0) in g.segs(t):
                            base = sq * (16 + g.seqlen) + 8 + tau0
                            ACT(pd_g[:, c2, base:base + n], PS(b, c0, c0 + n), AF.Copy, [pk(b)], ["pd"])
                    if pend is not None:
                        next(pend, None)
                    ckpt("b1")
                    w, wk = wload(d_win[l][:, 512:768], KC, 256)
                    for c4 in range(4):
                        cg = t * 4 + c4
                        b = rr_all()
                        for kc in range(KC):
                            MM(PS(b, 0, 256), hA[:, kc, c4 * 128:(c4 + 1) * 128], w[:, kc, :], kc == 0, kc == KC - 1, [wk, hk(kc)], [pk(b)])
                        ACT(va_g[:, cg, :], PS(b, 0, 128), AF.Copy, [pk(b)], ["va%d" % cg])
                        import os
                        dbgm = os.environ.get("DBGM", "full")
                        if not g.sample and dbgm != "nocopy":
                            if dbgm == "actcopy":
                                ACT(stg, PS(b, 0, 256), AF.Copy, [pk(b)], ["stg"])
                            else:
                                VCP(stg, PS(b, 0, 256), [pk(b)], ["stg"])
                            if dbgm not in ("nodma", "actcopy"):
                                DMA(o_cv[l, cg * 128:(cg + 1) * 128, :], stg[:, 0:128], ["stg"], [])
                                DMA(o_ck[l, cg * 128:(cg + 1) * 128, :], stg[:, 128:256], ["stg"], [])
                    if pend is not None:
                        for _ in pend:
                            pass
                    ckpt("b2")
                    w, wk = wload(d_win[l][:, 768:1024], KC, 256)
                    for c4 in range(4):
                        b = rr_all()
                        for kc in range(KC):
                            MM(PS(b, 0, 256), hA[:, kc, c4 * 128:(c4 + 1) * 128], w[:, kc, :], kc == 0, kc == KC - 1, [wk, hk(kc)], [pk(b)])
                        for h4 in range(4):
                            ACT(kdb1[:, c4, h4 * 64:(h4 + 1) * 64], PS(b, h4 * 64, (h4 + 1) * 64), AF.Identity, [pk(b), "kdec"], ["p1a%d" % c4],
                                scale=kdec3[:, 1, h4:h4 + 1])
                    ckpt("b3")
                    w, wk = wload(d_win[l][:, 1024:1280], KC, 256)
                    for c4 in range(4):
                        b = rr_all()
                        for kc in range(KC):
                            MM(PS(b, 0, 256), hA[:, kc, c4 * 128:(c4 + 1) * 128], w[:, kc, :], kc == 0, kc == KC - 1, [wk, hk(kc)], [pk(b)])
                        ACT(vcb1[:, c4, :], PS(b, 0, 256), AF.Copy, [pk(b)], ["p1b%d" % c4])
                    if i1 + 1 < len(order1):
                        tn_ = order1[i1 + 1]
                        make_h(hA, l, g, tn_, "a")
                        if g.sample:
                            DMA(ropeC, d_ropeC[:, tn_ * 512:(tn_ + 1) * 512], (), ["ropeC"])
                            DMA(ropeS, d_ropeS[:, tn_ * 512:(tn_ + 1) * 512], (), ["ropeS"])
                    def scan_gen(t=t):
                        for c4 in reversed(range(4)):
                            cg = t * 4 + c4
                            sq = cg // g.cps
                            if cg % g.cps == g.cps - 1:
                                if g.sample:
                                    VCP(SrunB, S0[:, 256:512], ["S0"], ["SrunB"])
                                else:
                                    VMEMSET(SrunB, 0.0, ["SrunB"])
                            ACT(Sb_g[:, cg, :], SrunB, AF.Copy, ["SrunB"], ["Sb%d" % cg])
                            b = rr_all()
                            for pr in range(2):
                                MM(PS(b, pr * 128, (pr + 1) * 128), kdb1[:, c4, pr * 128:(pr + 1) * 128], vcb1[:, c4, pr * 128:(pr + 1) * 128],
                                   True, True, ["p1a%d" % c4, "p1b%d" % c4], [pk(b)])
                            f, fk = fsc()
                            TT(f[:, 0:256], PS(b, 0, 256), bd2, ALU.mult, [pk(b), "bd2"], [fk])
                            for pr in range(2):
                                STT(SrunB3[:, pr, :], SrunB3[:, pr, :], cdec3[:, 1, pr:pr + 1], f[:, pr * 128:(pr + 1) * 128], ALU.mult, ALU.add,
                                    ["SrunB", "cdec", fk], ["SrunB"])
                            if (not g.sample) and cg % g.cps == 0:
                                state_out(SrunB3, "SrunB", l, sq, 1)
                            yield

                    pend = scan_gen()
                if pend is not None:
                    for _ in pend:
                        pass
                ckpt("p1_%s%d" % (g.name, l))
                VMEMSET(vz, 0.0, ["vz%d" % c for c in range(4)])
                for t in range(g.ntile):
                    tcs = slice(t * 512, (t + 1) * 512)
                    if t == 0:
                        make_h(hA, l, g, t, "a")
                        if g.sample:
                            DMA(ropeC, d_ropeC[:, tcs], (), ["ropeC"])
                            DMA(ropeS, d_ropeS[:, tcs], (), ["ropeS"])
                    w, wk = wload(d_win[l][:, 1280:1536], KC, 256)
                    if g.sample:
                        wp, wpk = wload(d_win[l][:, 1536:1792], KC, 256)
                    for gq in range(2):
                        b1 = rr_all()
                        for kc in range(KC):
                            MM(PS(b1), w[:, kc, gq * 128:(gq + 1) * 128], hA[:, kc, :], kc == 0, kc == KC - 1, [wk, hk(kc)], [pk(b1)])
                        if g.sample:
                            b2 = rr_all()
                            for kc in range(KC):
                                MM(PS(b2), wp[:, kc, gq * 128:(gq + 1) * 128], hA[:, kc, :], kc == 0, kc == KC - 1, [wpk, hk(kc)], [pk(b2)])
                            f1, f1k = fsc()
                            f2, f2k = fsc()
                            TT(f1, PS(b1), ropeC, ALU.mult, [pk(b1), "ropeC"], [f1k])
                            TT(f2, PS(b2), ropeS, ALU.mult, [pk(b2), "ropeS"], [f2k])
                            TT(qaT[:, gq, :], f1, f2, ALU.add, [f1k, f2k], ["qaT"])
                        else:
                            ACT(qaT[:, gq, :], PS(b1), AF.Copy, [pk(b1)], ["qaT"])
                    for (c0, n, sq, tau0) in g.segs(t):
                        base = sq * (16 + g.seqlen) + tau0
                        for c2 in range(2):
                            A_, Ak = fsc()
                            B_, Bk = fsc()
                            M_, Mk = fsc()
                            hseg = [(0, n)] if n + 16 <= 512 else [(0, n // 2), (n // 2, n - n // 2)]
                            for (o0, nn) in hseg:
                                Pp = pd_g[:, c2, base + o0:base + o0 + nn + 16]
                                W_ = nn + 16
                                TT(A_[:, 1:W_], Pp[:, 0:W_ - 1], Pp[:, 1:W_], ALU.add, ["pd"], [Ak])
                                TT(B_[:, 2:W_ - 1], A_[:, 1:W_ - 2], A_[:, 3:W_], ALU.add, [Ak], [Bk])
                                if c2 == 0:
                                    TS(M_[0:64, 0:nn], A_[0:64, 8:8 + nn], invw[0:64, 0:1], None, ALU.mult, ALU.bypass, [Ak, "csm"], [Mk])
                                    TS(M_[64:128, 0:nn], B_[64:128, 8:8 + nn], invw[64:128, 0:1], None, ALU.mult, ALU.bypass, [Bk, "csm"], [Mk])
                                else:
                                    TT(A_[:, 4:W_ - 3], B_[:, 2:W_ - 5], B_[:, 6:W_ - 1], ALU.add, [Bk, Ak], [Ak])
                                    TT(B_[64:128, 8:W_ - 8], A_[64:128, 4:W_ - 12], A_[64:128, 12:W_ - 4], ALU.add, [Ak, Bk], [Bk])
                                    TS(M_[0:64, 0:nn], A_[0:64, 8:8 + nn], invw[0:64, 1:2], None, ALU.mult, ALU.bypass, [Ak, "csm"], [Mk])
                                    TS(M_[64:128, 0:nn], B_[64:128, 8:8 + nn], invw[64:128, 1:2], None, ALU.mult, ALU.bypass, [Bk, "csm"], [Mk])
                                if tau0 + o0 == 0:
                                    TT(M_[:, 0:8], M_[:, 0:8], corrS[:, c2, :], ALU.mult, [Mk, "csm"], [Mk])
                                if tau0 + o0 + nn == g.seqlen:
                                    TT(M_[:, nn - 8:nn], M_[:, nn - 8:nn], corrE[:, c2, :], ALU.mult, [Mk, "csm"], [Mk])
                                df, dfk = bsc()
                                TT(df[:, 0:nn], M_[:, 0:nn], Pp[:, 8:8 + nn], ALU.subtract, [Mk, "pd"], [dfk])
                                b = rr_all()
                                MM(PS(b, 0, nn), pw4[:, l, c2, :], df[:, 0:nn], True, True, ["poolW", dfk], [pk(b)])
                                ACT(yD[:, c2, c0 + o0:c0 + o0 + nn], PS(b, 0, nn), AF.Identity, [pk(b), "pscale"], ["p1b%d" % (2 * c2), "p1b%d" % (2 * c2 + 1)],
                                    scale=pscale3[:, l, c2:c2 + 1])

                    def proj_gen():
                        w, wk = wload(d_win[l][:, 1792:2048], KC, 256)
                        for h4 in range(4):
                            b = pbank()
                            for kc in range(KC):
                                MM(PS(b)[0:64, :], w[:, kc, h4 * 64:(h4 + 1) * 64], hA[:, kc, :], kc == 0, kc == KC - 1, [wk, hk(kc)], [pk(b)])
                            ACT(ubT[0:64, h4, :], PS(b)[0:64, :], AF.Copy, [pk(b)], ["ubT"])
                        yield
                        w, wk = wload(d_win[l][:, 2048:2304], KC, 256)
                        for pr in range(2):
                            b = pbank()
                            for kc in range(KC):
                                MM(PS(b), w[:, kc, pr * 128:(pr + 1) * 128], hA[:, kc, :], kc == 0, kc == KC - 1, [wk, hk(kc)], [pk(b)])
                            ACT(qcT[:, pr, :], PS(b), AF.Copy, [pk(b)], ["qcT"])
                            TT(v3(qdf[:, pr, :], 4), v3(PS(b), 4), qdecF3[:, pr, :].unsqueeze(1).to_broadcast([128, 4, 128]), ALU.mult,
                               [pk(b), "qdecF"], ["qdf"])
                            TT(v3(qdb[:, pr, :], 4), v3(PS(b), 4), qdecB3[:, pr, :].unsqueeze(1).to_broadcast([128, 4, 128]), ALU.mult,
                               [pk(b), "qdecB"], ["qdb"])
                        yield
                        w, wk = wload(d_win[l][:, 2304:2560], KC, 256)
                        for pr in range(2):
                            b = pbank()
                            for kc in range(KC):
                                MM(PS(b), w[:, kc, pr * 128:(pr + 1) * 128], hA[:, kc, :], kc == 0, kc == KC - 1, [wk, hk(kc)], [pk(b)])
                            ACT(kcT[:, pr, :], PS(b), AF.Copy, [pk(b)], ["kcT"])
                        yield
                        for (c0, dstg, key) in ((2560, sgf, "sgf"), (2816, sgb, "sgb")):
                            w, wk = wload(d_win[l][:, c0:c0 + 256], KC, 256)
                            for pr in range(2):
                                b = pbank()
                                for kc in range(KC):
                                    MM(PS(b), w[:, kc, pr * 128:(pr + 1) * 128], hA[:, kc, :], kc == 0, kc == KC - 1, [wk, hk(kc)], [pk(b)])
                                ACT(dstg[:, pr, :], PS(b), AF.Silu, [pk(b)], [key])
                            yield
                        w, wk = wload(d_win[l][:, 3072:3328], KC, 256)
                        for c4 in range(4):
                            b = pbank()
                            for kc in range(KC):
                                MM(PS(b, 0, 256), hA[:, kc, c4 * 128:(c4 + 1) * 128], w[:, kc, :], kc == 0, kc == KC - 1, [wk, hk(kc)], [pk(b)])
                            f, fk = fsc()
                            S.add("dve", lambda h, o=f[:, 256:262], i=PS(b, 0, 256): h.bn_stats(out=o, in_=i), [pk(b)], [fk])
                            S.add("dve", lambda h, o=f[:, 262:264], i=f[:, 256:262]: h.bn_aggr(out=o, in_=i), [fk], [fk])
                            ACT(f[:, 264:265], f[:, 263:264], AF.Ln, [fk], [fk], bias=EPS)
                            ACT(f[:, 264:265], f[:, 264:265], AF.Exp, [fk], [fk], scale=-0.5)
                            TS(f[:, 0:256], PS(b, 0, 256), f[:, 262:263], f[:, 264:265], ALU.subtract, ALU.mult, [pk(b), fk], [fk])
                            TT(f[:, 0:256], f[:, 0:256], nwb[:, 0:256], ALU.mult, [fk, "nwb"], [fk])
                            TT(vbn[:, c4, :], f[:, 0:256], nwb[:, 256:512], ALU.add, [fk, "nwb"], ["vbn%d" % c4])
                        yield
                        w, wk = wload(d_win[l][:, 3328:3584], KC, 256)
                        for c4 in range(4):
                            b = pbank()
                            for kc in range(KC):
                                MM(PS(b, 0, 256), hA[:, kc, c4 * 128:(c4 + 1) * 128], w[:, kc, :], kc == 0, kc == KC - 1, [wk, hk(kc)], [pk(b)])
                            for h4 in range(4):
                                ACT(kdf[:, c4, h4 * 64:(h4 + 1) * 64], PS(b, h4 * 64, (h4 + 1) * 64), AF.Identity, [pk(b), "kdec"], ["kdf%d" % c4],
                                    scale=kdec3[:, 0, h4:h4 + 1])
                        yield
                        w, wk = wload(d_win[l][:, 3584:3840], KC, 256)
                        for c4 in range(4):
                            b = pbank()
                            for kc in range(KC):
                                MM(PS(b, 0, 256), hA[:, kc, c4 * 128:(c4 + 1) * 128], w[:, kc, :], kc == 0, kc == KC - 1, [wk, hk(kc)], [pk(b)])
                            ACT(vcb[:, c4, :], PS(b, 0, 256), AF.Copy, [pk(b)], ["vcb%d" % c4])
                            psv = PS(b, 0, 256).rearrange("p (a h e) -> p a h e", a=2, h=2)
                            for hh in range(2):
                                ACT(vz5[:, c4, :, hh, hh * 64:(hh + 1) * 64], psv[:, :, hh, :], AF.Copy, [pk(b)], ["vz%d" % c4])
                        yield

                    def att_keys(qi):
                        cq = t * 4 + qi
                        if g.sample:
                            kl = []
                            if cq - 1 >= 0:
                                kl.append(("loc", cq - 1, mprev, "cbf"))
                            kl.append(("loc", cq, None, None))
                            if cq + 1 < g.nchunk:
                                kl.append(("loc", cq + 1, mnext, "cbf"))
                            kl += [("ctx", 0, None, None), ("ctx", 1, None, None)]
                        else:
                            sq = cq // g.cps
                            kl = [("loc", sq * g.cps + j, None, None) for j in range(g.cps)]
                        return kl

                    def att_scores(qi, kvh):
                        rows = slice(kvh * 64, (kvh + 1) * 64)
                        out = []
                        for (kind, ck_, msk, mk) in att_keys(qi):
                            bS = rr_st()
                            if kind == "loc":
                                kT = kaT_g[rows, ck_ * 128:(ck_ + 1) * 128]
                                kkey = "kaT%d" % (ck_ // 4)
                                vv = va_g[:, ck_, kvh * 64:(kvh + 1) * 64]
                                vkey = "va%d" % ck_
                            else:
                                kT = kctx3[rows, l, ck_ * 128:(ck_ + 1) * 128]
                                kkey = "kctx"
                                vv = vctx4[:, l, ck_, kvh * 64:(kvh + 1) * 64]
                                vkey = "vctx"
                            MM(PS(bS, 0, 256), kT, qaT[rows, :, qi * 128:(qi + 1) * 128], True, True, [kkey, "qaT"], [pk(bS)])
                            ei = er_i[0] % NER
                            er_i[0] += 1
                            E, Ek = ering[ei], "er%d" % ei
                            ACT(E, PS(bS, 0, 256), AF.Exp, [pk(bS)], [Ek], scale=0.125)
                            if msk is not None:
                                TT(E, E, msk, ALU.mult, [Ek, mk], [Ek])
                            out.append((E, Ek, vv, vkey))
                        return out

                    def att_out(qi, kvh, it, es_):
                        bN, bD = 5, 6
                        for n, (E, Ek, vv, vkey) in enumerate(es_):
                            MM(PS(bN, 0, 256)[0:64, :], vv, E, n == 0, n == len(es_) - 1, [vkey, Ek], [pk(bN)])
                            MM(PS(bD, 0, 256)[0:64, :], ones64, E, n == 0, n == len(es_) - 1, ["cbf", Ek], [pk(bD)])
                        f, fk = fsc()
                        for gq in range(2):
                            h4 = kvh * 2 + gq
                            ACT(f[0:64, gq * 128:(gq + 1) * 128], PS(bD, gq * 128, (gq + 1) * 128)[0:64, :], AF.Ln, [pk(bD), "esink"], [fk],
                                bias=esink3[0:64, l, h4:h4 + 1])
                        ACT(f[0:64, 0:256], f[0:64, 0:256], AF.Exp, [fk], [fk], scale=-1.0)
                        for gq in range(2):
                            TT(yA[gq * 64:(gq + 1) * 64, kvh, qi * 128:(qi + 1) * 128], PS(bN, gq * 128, (gq + 1) * 128)[0:64, :],
                               f[0:64, gq * 128:(gq + 1) * 128], ALU.mult, [pk(bN), fk], ["p1a%d" % (2 * kvh), "p1a%d" % (2 * kvh + 1)])

                    pbank = RR([0, 1, 2, 7])
                    rr_st = RR([3, 4])
                    gen = proj_gen()
                    its = [(qi, kvh) for qi in range(4) for kvh in range(2)]
                    nxt = att_scores(*its[0])
                    for it, (qi, kvh) in enumerate(its):
                        next(gen, None)
                        cur = nxt
                        if it + 1 < len(its):
                            nxt = att_scores(*its[it + 1])
                        att_out(qi, kvh, it, cur)
                    for _ in gen:
                        pass
                    ckpt("proj")

                    ckpt("attn")
                    for h4 in range(4):
                        b = rr_all()
                        for c4 in range(4):
                            MM(PS(b, c4 * 128, (c4 + 1) * 128)[0:64, :], vbn[:, c4, h4 * 64:(h4 + 1) * 64], wsT4[:, l, h4, :], True, True,
                               ["vbn%d" % c4, "wsT"], [pk(b)])
                        f, fk = fsc()
                        TT(v3(f[0:64, :], 4), v3(PS(b)[0:64, :], 4), bsT3[0:64, h4, :].unsqueeze(1).to_broadcast([64, 4, 128]), ALU.add,
                           [pk(b), "bsT"], [fk])
                        pr, hh = h4 // 2, h4 % 2
                        TT(yT[hh * 64:(hh + 1) * 64, 2 + pr, :], f[0:64, :], ubT[0:64, h4, :], ALU.mult, [fk, "ubT"], [hk(2 + pr)])

                    ckpt("sgu")
                    bO = {(0, 0): 4, (0, 1): 5, (1, 0): 6, (1, 1): 7}
                    for c4 in range(4):
                        cg = t * 4 + c4
                        sq = cg // g.cps
                        if cg % g.cps == 0:
                            if g.sample:
                                VCP(SrunF, S0[:, 0:256], ["S0"], ["SrunF"])
                            else:
                                VMEMSET(SrunF, 0.0, ["SrunF"])
                        ACT(sfr[c4], SrunF, AF.Copy, ["SrunF"], ["sfr%d" % c4])
                        bC = rr_lo()
                        for pr in range(2):
                            MM(PS(bC, pr * 128, (pr + 1) * 128), kdf[:, c4, pr * 128:(pr + 1) * 128], vcb[:, c4, pr * 128:(pr + 1) * 128],
                               True, True, ["kdf%d" % c4, "vcb%d" % c4], [pk(bC)])
                        f, fk = fsc()
                        TT(f[:, 0:256], PS(bC, 0, 256), bd2, ALU.mult, [pk(bC), "bd2"], [fk])
                        for pr in range(2):
                            STT(SrunF3[:, pr, :], SrunF3[:, pr, :], cdec3[:, 0, pr:pr + 1], f[:, pr * 128:(pr + 1) * 128], ALU.mult, ALU.add,
                                ["SrunF", "cdec", fk], ["SrunF"])
                        if (not g.sample) and cg % g.cps == g.cps - 1:
                            state_out(SrunF3, "SrunF", l, sq, 0)

                    def ret_scores(c4):
                        cc = slice(c4 * 128, (c4 + 1) * 128)
                        bSh = [rr_lo(), rr_lo()]
                        for h4 in range(4):
                            pr, hh = h4 // 2, h4 % 2
                            rws = slice(hh * 64, (hh + 1) * 64)
                            MM(PS(bSh[hh], pr * 128, (pr + 1) * 128), kcT[rws, pr, cc], qcT[rws, pr, cc], True, True, ["kcT", "qcT"], [pk(bSh[hh])])
                        sf, sb_ = scf[c4 % 2], scb[c4 % 2]
                        sfk, sbk = "scf%d" % (c4 % 2), "scb%d" % (c4 % 2)
                        sf4 = sf.rearrange("p (a h i) -> p a h i", a=2, h=2)
                        sb4 = sb_.rearrange("p (a h i) -> p a h i", a=2, h=2)
                        Df4 = DfT.rearrange("p (a h i) -> p a h i", a=2, h=2)
                        Db4 = DbT.rearrange("p (a h i) -> p a h i", a=2, h=2)
                        for hh in range(2):
                            TT(sf4[:, :, hh, :], v3(PS(bSh[hh], 0, 256), 2), Df4[:, :, hh, :], ALU.mult, [pk(bSh[hh]), "DfT"], [sfk])
                            TT(sb4[:, :, hh, :], v3(PS(bSh[hh], 0, 256), 2), Db4[:, :, hh, :], ALU.mult, [pk(bSh[hh]), "DbT"], [sbk])

                    def ret_out(c4):
                        cg = t * 4 + c4
                        cc = slice(c4 * 128, (c4 + 1) * 128)
                        sf, sb_ = scf[c4 % 2], scb[c4 % 2]
                        sfk, sbk = "scf%d" % (c4 % 2), "scb%d" % (c4 % 2)
                        for dr in range(2):
                            sc_, sck = (sf, sfk) if dr == 0 else (sb_, sbk)
                            qd = qdf if dr == 0 else qdb
                            qdk = "qdf" if dr == 0 else "qdb"
                            for pr in range(2):
                                b = bO[(dr, pr)]
                                for hh in range(2):
                                    h4 = pr * 2 + hh
                                    MM(PS(b, c4 * 128, (c4 + 1) * 128), vz5[:, c4, pr, hh, :], sc_[:, h4 * 128:(h4 + 1) * 128], hh == 0, False,
                                       ["vz%d" % c4, sck], [pk(b)])
                                if dr == 0:
                                    st_ap, stk = sfr[c4][:, pr * 128:(pr + 1) * 128], "sfr%d" % c4
                                else:
                                    st_ap, stk = Sb_g[:, cg, pr * 128:(pr + 1) * 128], "Sb%d" % cg
                                MM(PS(b, c4 * 128, (c4 + 1) * 128), st_ap, qd[:, pr, cc], False, True, [stk, qdk], [pk(b)])

                    ret_scores(0)
                    for c4 in range(4):
                        if c4 + 1 < 4:
                            ret_scores(c4 + 1)
                        ret_out(c4)
                    for pr in range(2):
                        ydir = []
                        for dr in range(2):
                            b = bO[(dr, pr)]
                            o32, o32k = fsc()
                            ob, obk = bsc()
                            oq, oqk = bsc()
                            ACT(o32, PS(b), AF.Copy, [pk(b)], [o32k])
                            ACT(ob, PS(b), AF.Copy, [pk(b)], [obk])
                            ACT(oq, PS(b), AF.Square, [pk(b)], [oqk])
                            bM, bQ = rr_lo(), rr_lo()
                            MM(PS(bM), bd64, ob, True, True, ["cbf", obk], [pk(bM)])
                            MM(PS(bQ), bd64, oq, True, True, ["cbf", oqk], [pk(bQ)])
                            fv, fvk = fsc()
                            ACT(fv, PS(bM), AF.Square, [pk(bM)], [fvk])
                            TT(fv, PS(bQ), fv, ALU.subtract, [pk(bQ), fvk], [fvk])
                            ACT(fv, fv, AF.Ln, [fvk], [fvk], bias=EPS)
                            ACT(fv, fv, AF.Exp, [fvk], [fvk], scale=-0.5)
                            TT(o32, o32, PS(bM), ALU.subtract, [o32k, pk(bM)], [o32k])
                            TT(o32, o32, fv, ALU.mult, [o32k, fvk], [o32k])
                            sg = sgf if dr == 0 else sgb
                            STT(o32, o32, gnw4[:, l, dr, pr:pr + 1], sg[:, pr, :], ALU.mult, ALU.mult, [o32k, "gnw", "sgf" if dr == 0 else "sgb"], [o32k])
                            ydir.append((o32, o32k))
                        TT(yT[:, 4 + pr, :], ydir[0][0], ydir[1][0], ALU.add, [ydir[0][1], ydir[1][1]], [hk(4 + pr)])

                    ckpt("ret")
                    ckpt("poolmix")
                    ga = mvec(l, g.ci, 2)
                    rr6 = RR(range(6))
                    for blk in range(4):
                        w, wk = wload(d_wout[l][:, blk * 256:(blk + 1) * 256], KC, 256)
                        for j in range(2):
                            m = blk * 2 + j
                            b = rr6()
                            for kc in range(KC):
                                if kc < 2:
                                    rhs_, rk_ = yA[:, kc, :], ["p1a%d" % (2 * kc), "p1a%d" % (2 * kc + 1)]
                                elif kc >= 6:
                                    rhs_, rk_ = yD[:, kc - 6, :], ["p1b%d" % (2 * (kc - 6)), "p1b%d" % (2 * (kc - 6) + 1)]
                                else:
                                    rhs_, rk_ = yT[:, kc, :], [hk(kc)]
                                MM(PS(b), w[:, kc, j * 128:(j + 1) * 128], rhs_, kc == 0, kc == KC - 1, [wk] + rk_, [pk(b)])
                            STT(X[:, m, tcs], PS(b), ga[:, m:m + 1], X[:, m, tcs], ALU.mult, ALU.add, [pk(b), "modv%d" % l, xk(m, t)], [xk(m, t)])
                            if m >= 2:
                                ln_stats(t, m - 2, 6, 7)
                    ln_stats(t, KC - 2, 6, 7)
                    ln_stats(t, KC - 1, 6, 7)
                    if t + 1 < g.ntile:
                        make_h(hA, l, g, t + 1, "a")
                        if g.sample:
                            tn = slice((t + 1) * 512, (t + 2) * 512)
                            DMA(ropeC, d_ropeC[:, tn], (), ["ropeC"])
                            DMA(ropeS, d_ropeS[:, tn], (), ["ropeS"])
                    ln_finish(t, lvec3[:, l, 0:8], lvec3[:, l, 8:16], 6, 7)

                ckpt("s1_%s%d" % (g.name, l))
                if (not g.sample) and l == 0:
                    for l2 in range(1, DEPTH):
                        do_mod(l2)
                S.barrier()
                T = 2 if g.sample else 1
                hcols = []
                if g.sample:
                    for gi in range(g.ntile // T):
                        if gi * T * 512 > 0:
                            hcols.append(gi * T * 512 - 1)
                        if (gi + 1) * T * 512 < g.nt:
                            hcols.append((gi + 1) * T * 512)
                    sv = dvec4[:, l, g.ci, KC:2 * KC]
                    bv = mvec(l, g.ci, 3)
                    for hi, col in enumerate(hcols):
                        for kc in range(KC):
                            ACT(hFh[:, kc, hi:hi + 1], X[:, kc, col:col + 1], AF.Identity, [xk(kc, col // 512), "dvec%d" % l, "modv%d" % l], ["hFh"],
                                scale=sv[:, kc:kc + 1], bias=bv[:, kc:kc + 1])
                for gi in range(g.ntile // T):
                    tiles = [gi * T + j for j in range(T)]
                    gc0 = tiles[0] * 512
                    W_ = T * 512
                    hkeys = ["hF%d" % kc for kc in range(KC)]
                    if gi == 0:
                        for j, t in enumerate(tiles):
                            make_h(hF[:, :, j * 512:(j + 1) * 512], l, g, t, "f", dkeys=hkeys)
                    halo = []
                    if g.sample:
                        if gc0 > 0:
                            halo.append(("L", hcols.index(gc0 - 1)))
                        if gc0 + W_ < g.nt:
                            halo.append(("R", hcols.index(gc0 + W_)))
                    pf = [wload(d_up[l][:, i_ * 256:(i_ + 1) * 256], KC, 256) for i_ in range(min(NSLOT, NFC))]
                    for i in range(NFC):
                        w, wk = pf.pop(0)
                        pb = {}
                        for j in range(T):
                            for ag in range(2):
                                b = rr_all()
                                pb[(j, ag)] = b
                                for kc in range(KC):
                                    MM(PS(b), w[:, kc, ag * 128:(ag + 1) * 128], hF[:, kc, j * 512:(j + 1) * 512], kc == 0, kc == KC - 1,
                                       [wk, hkeys[kc]], [pk(b)])
                        bH = None
                        if halo:
                            bH = rr_all()
                            for ag in range(2):
                                for k_, (s_, hx) in enumerate(halo):
                                    for kc in range(KC):
                                        MM(PS(bH, ag * 2 + k_, ag * 2 + k_ + 1), w[:, kc, ag * 128:(ag + 1) * 128], hFh[:, kc, hx:hx + 1],
                                           kc == 0, kc == KC - 1, [wk, "hFh"], [pk(bH)])
                        if i + NSLOT < NFC:
                            pf.append(wload(d_up[l][:, (i + NSLOT) * 256:(i + NSLOT + 1) * 256], KC, 256))
                        accs = {}
                        for j in range(T):
                            t = tiles[j]
                            for ag in range(2):
                                ch = ag * NFC + i
                                cw0 = convT4[:, l, 0, ch:ch + 1]
                                cw1 = convT4[:, l, 1, ch:ch + 1]
                                cw2 = convT4[:, l, 2, ch:ch + 1]
                                cbv = convT4[:, l, 3, ch:ch + 1]
                                b = pb[(j, ag)]
                                acc, acck = fsc()
                                accs[(j, ag)] = (acc, acck)
                                ACT(acc, PS(b), AF.Identity, [pk(b), "convT"], [acck], scale=cw1, bias=cbv)
                                for (c0, n, sq, tau0) in g.segs(t):
                                    STT(acc[:, c0 + 1:c0 + n], PS(b, c0, c0 + n - 1), cw0, acc[:, c0 + 1:c0 + n], ALU.mult, ALU.add,
                                        [pk(b), "convT", acck], [acck])
                                    STT(acc[:, c0:c0 + n - 1], PS(b, c0 + 1, c0 + n), cw2, acc[:, c0:c0 + n - 1], ALU.mult, ALU.add,
                                        [pk(b), "convT", acck], [acck])
                                if g.sample:
                                    if j > 0:
                                        bp = pb[(j - 1, ag)]
                                        STT(acc[:, 0:1], PS(bp, 511, 512), cw0, acc[:, 0:1], ALU.mult, ALU.add, [pk(bp), "convT", acck], [acck])
                                    elif gc0 > 0:
                                        hi = [k for k, (s_, _) in enumerate(halo) if s_ == "L"][0]
                                        STT(acc[:, 0:1], PS(bH, ag * 2 + hi, ag * 2 + hi + 1), cw0, acc[:, 0:1], ALU.mult, ALU.add,
                                            [pk(bH), "convT", acck], [acck])
                                    if j < T - 1:
                                        bn_ = pb[(j + 1, ag)]
                                        STT(acc[:, 511:512], PS(bn_, 0, 1), cw2, acc[:, 511:512], ALU.mult, ALU.add, [pk(bn_), "convT", acck], [acck])
                                    elif gc0 + W_ < g.nt:
                                        hi = [k for k, (s_, _) in enumerate(halo) if s_ == "R"][0]
                                        STT(acc[:, 511:512], PS(bH, ag * 2 + hi, ag * 2 + hi + 1), cw2, acc[:, 511:512], ALU.mult, ALU.add,
                                            [pk(bH), "convT", acck], [acck])
                            aa, aak = accs[(j, 0)]
                            gg_, ggk = accs[(j, 1)]
                            ACT(aa, aa, AF.Silu, [aak], [aak])
                            S.add("pool", lambda h, o=hid[:, i, j * 512:(j + 1) * 512], a_=aa, b_=gg_: h.tensor_tensor(out=o, in0=a_, in1=b_, op=ALU.mult),
                                  [aak, ggk], ["hid%d" % i])
                    gf = mvec(l, g.ci, 5)
                    rr4 = RR(range(4))
                    prev = None
                    for m in range(KC):
                        w, wk = wload(d_down[l][:, m * 128:(m + 1) * 128], NFC, 128)
                        for j, t in enumerate(tiles):
                            b = rr4()
                            for k in range(NFC):
                                MM(PS(b), w[:, k, :], hid[:, k, j * 512:(j + 1) * 512], k == 0, k == NFC - 1, [wk, "hid%d" % k], [pk(b)])
                            tcs = slice(t * 512, (t + 1) * 512)
                            STT(X[:, m, tcs], PS(b), gf[:, m:m + 1], X[:, m, tcs], ALU.mult, ALU.add, [pk(b), "modv%d" % l, xk(m, t)], [xk(m, t)])
                        if prev is not None:
                            for j, t in enumerate(tiles):
                                ln_stats(t, prev, 4 + 2 * j, 5 + 2 * j)
                        prev = m
                    for j, t in enumerate(tiles):
                        ln_stats(t, prev, 4 + 2 * j, 5 + 2 * j)
                    if gi + 1 < g.ntile // T:
                        for j2 in range(T):
                            make_h(hF[:, :, j2 * 512:(j2 + 1) * 512], l, g, (gi + 1) * T + j2, "f", dkeys=hkeys)
                    for j, t in enumerate(tiles):
                        ln_finish(t, lvec3[:, l, 16:24], lvec3[:, l, 24:32], 4 + 2 * j, 5 + 2 * j)
                        if last:
                            for kc in range(KC):
                                DMA(o_y[kc * 128:(kc + 1) * 128, t * 512:(t + 1) * 512], X[:, kc, t * 512:(t + 1) * 512], [xk(kc, t)], [])
                S.barrier()
    except _Stop:
        pass

    S.emit(nc, block, semobj)
    es.close()
    return nc


_CACHE = {}


def _prep_inputs(inp):
    f32 = lambda a: np.ascontiguousarray(np.asarray(a, dtype=np.float32))
    x_prompt = f32(inp["x_prompt"])
    x_sample = f32(inp["x_sample"])
    ck = f32(inp["cache_attn_k"])
    cv = f32(inp["cache_attn_v"])
    st = f32(inp["state_ret"])
    c = f32(inp["c"])
    c_ctx = f32(inp["c_ctx"])
    w_in = f32(inp["w_in"])
    big, small, bd2, cb = _const_tables()
    ropeC, ropeS = _rope_tables()
    shared = {
        "w_ada": f32(inp["w_ada"]),
        "b_adaT": np.ascontiguousarray(np.concatenate([_fm(f32(inp["b_ada"])[l], 48) for l in range(DEPTH)], axis=1)),
        "w_in_ext": np.ascontiguousarray(w_in[:, :, _win_ext_cols()]),
        "w_out": f32(inp["w_out"]),
        "ffn_up_ext": np.ascontiguousarray(f32(inp["ffn_up"])[:, :, _up_ext_cols()]),
        "ffn_down": f32(inp["ffn_down"]),
        "c_big": big, "c_small": small, "c_bd2": bd2, "c_bf": cb, "ropeC": ropeC, "ropeS": ropeS,
    }
    cw = f32(inp["ffn_conv_w"])
    cbias = f32(inp["ffn_conv_b"])
    conv = np.zeros((128, DEPTH, 4, 44), np.float32)
    for l in range(DEPTH):
        for j in range(3):
            conv[:, l, j, :] = _fm(cw[l, j], 44)
        conv[:, l, 3, :] = _fm(cbias[l], 44)
    shared["convT"] = conv.reshape(128, -1)
    lnw, lnb = f32(inp["ln_w"]), f32(inp["ln_b"])
    ln = np.zeros((128, DEPTH, 2, 2, KC), np.float32)
    for l in range(DEPTH):
        for i in range(2):
            ln[:, l, i, 0, :] = _fm(lnw[l, i], KC)
            ln[:, l, i, 1, :] = _fm(lnb[l, i], KC)
    shared["lnT"] = ln.reshape(128, -1)
    shared["sink"] = f32(inp["attn_sink"]).reshape(1, -1)
    shared["sgu_nwb"] = np.ascontiguousarray(np.concatenate([f32(inp["sgu_norm_w"]), f32(inp["sgu_norm_b"])], axis=1))
    ws = f32(inp["sgu_ws"])
    shared["wsT"] = np.ascontiguousarray(ws.transpose(3, 0, 1, 2)).reshape(128, -1)
    bs = f32(inp["sgu_bs"])
    shared["bsT"] = np.ascontiguousarray(np.broadcast_to(bs[None], (64,) + bs.shape)).reshape(64, -1)
    dec = f32(inp["ret_decay"])
    shared["dec_row"] = dec.reshape(1, -1)
    dcol = np.zeros((128, DEPTH, 2, 2), np.float32)
    for hh in range(2):
        for pr in range(2):
            dcol[hh * 64:(hh + 1) * 64, :, :, pr] = dec[None, :, :, pr * 2 + hh]
    shared["dec_col"] = dcol.reshape(128, -1)
    gn = f32(inp["ret_gn_w"])
    gcol = np.zeros((128, DEPTH, 2, 2), np.float32)
    for l in range(DEPTH):
        for dr in range(2):
            gcol[:, l, dr, :] = _fm(gn[l, dr], 2)
    shared["gnw"] = gcol.reshape(128, -1)
    shared["poolw"] = f32(inp["pool_w"])
    psc = f32(inp["pool_scale"])
    shared["pool_scaleT"] = np.ascontiguousarray(np.concatenate([_fm(psc[l], 2) for l in range(DEPTH)], axis=1))

    in_maps = []
    for core in range(NCORE):
        b = core % 2
        m = dict(shared)
        xp = x_prompt[2 * core:2 * core + 2].reshape(512, D)
        m["xpT"] = np.ascontiguousarray(xp.T)
        m["xsT"] = np.ascontiguousarray(x_sample[b].T)
        cond = np.stack([c_ctx, c[b]], axis=1)
        m["cond"] = np.ascontiguousarray(cond.reshape(KC, 128, 2).transpose(1, 0, 2)).reshape(128, -1)
        kk = ck[b].reshape(DEPTH, 256, 128)
        m["kctxT"] = np.ascontiguousarray(kk.transpose(2, 0, 1)).reshape(128, -1)
        vv = cv[b].reshape(DEPTH, 2, 128, 128)
        m["vctx"] = np.ascontiguousarray(vv.transpose(2, 0, 1, 3)).reshape(128, -1)
        m["state0"] = np.ascontiguousarray(st[b])
        in_maps.append(m)
    return in_maps


def kernel(**inputs):
    if "nc" not in _CACHE:
        _CACHE["nc"] = build_program()
    nc = _CACHE["nc"]
    in_maps = _prep_inputs(inputs)
    res = run_bass_kernel_spmd(nc, in_maps, core_ids=list(range(NCORE)))
    R = res.results
    B, SEQ = 16, 256
    y_prompt = np.zeros((B, SEQ, D), np.float32)
    new_k = np.zeros((B, DEPTH, SEQ, 2, 64), np.float32)
    new_v = np.zeros((B, DEPTH, SEQ, 2, 64), np.float32)
    new_s = np.zeros((B, DEPTH, 2, 4, 64, 64), np.float32)
    y_sample = np.zeros((2, 2048, D), np.float32)
    for core in range(NCORE):
        r = R[core]
        yp = np.asarray(r["ypT"]).T.reshape(2, SEQ, D)
        y_prompt[2 * core:2 * core + 2] = yp
        ckk = np.asarray(r["ck_out"]).reshape(DEPTH, 2, SEQ, 2, 64)
        cvv = np.asarray(r["cv_out"]).reshape(DEPTH, 2, SEQ, 2, 64)
        new_k[2 * core:2 * core + 2] = ckk.transpose(1, 0, 2, 3, 4)
        new_v[2 * core:2 * core + 2] = cvv.transpose(1, 0, 2, 3, 4)
        so = np.asarray(r["st_out"])
        new_s[2 * core:2 * core + 2] = so.transpose(1, 0, 2, 3, 4, 5)
        if core < 2:
            y_sample[core] = np.asarray(r["ysT"]).T
    return (y_prompt, y_sample, new_k, new_v, new_s)
```
